# Optimizing a Trainium2 kernel written in Bass

```python
import jax, jax.numpy as jnp
from jax import lax
import numpy as np

D_MODEL = 2048
BATCH = 1
SEQ = 8192
DEPTH = 4

CHUNK = 64
N_META = 16
Q_BLOCK = 128
MLA_HEADS = 8
MLA_Q_LORA = 512
MLA_KV_LORA = 512
MLA_NOPE = 128
MLA_ROPE = 64
MLA_V = 128
ROPE_THETA = 10000.0
FOX_HEADS = 8
FOX_HD = 128
FOX_W = FOX_HEADS * FOX_HD
FORGET_BIAS = 3.0
MIX_WIDTH = MLA_HEADS * MLA_V + FOX_W
D_FF = 5632
CONV_K = 3
EPS = 1e-6
NEG = -1e30
IN_SPLIT_SIZES = (MLA_Q_LORA, MLA_KV_LORA, MLA_ROPE, FOX_W, FOX_W, FOX_W, FOX_W, FOX_HEADS)
IN_COLS = sum(IN_SPLIT_SIZES)

kernel_name = "hybrid_mla_fox_convffn_trunk"


def rms_norm(x, g):
    xf = x.astype(jnp.float32)
    y = xf * lax.rsqrt(jnp.mean(xf * xf, axis=-1, keepdims=True) + EPS)
    return (y * g.astype(jnp.float32)).astype(x.dtype)


def apply_rope(x, cos, sin):
    xf = x.astype(jnp.float32)
    half = xf.shape[-1] // 2
    x1, x2 = xf[..., :half], xf[..., half:]
    out = jnp.concatenate([x1 * cos - x2 * sin, x2 * cos + x1 * sin], axis=-1)
    return out.astype(x.dtype)


def mla_mixer(c_q, c_kv, k_rope, g_q, g_kv, w_q_up, w_kv_up, cos, sin, chunk_id):
    B, L, _ = c_q.shape
    q = (rms_norm(c_q, g_q) @ w_q_up).reshape(B, L, MLA_HEADS, MLA_NOPE + MLA_ROPE)
    q_nope = q[..., :MLA_NOPE]
    q_rope = apply_rope(q[..., MLA_NOPE:], cos[None, :, None], sin[None, :, None])
    kv = (rms_norm(c_kv, g_kv) @ w_kv_up).reshape(B, L, MLA_HEADS, MLA_NOPE + MLA_V)
    k_nope, v = kv[..., :MLA_NOPE], kv[..., MLA_NOPE:]
    k_r = apply_rope(k_rope, cos[None], sin[None])
    scale = (MLA_NOPE + MLA_ROPE) ** -0.5

    def one_block(start):
        qn = lax.dynamic_slice_in_dim(q_nope, start, Q_BLOCK, axis=1)
        qr = lax.dynamic_slice_in_dim(q_rope, start, Q_BLOCK, axis=1)
        cq = lax.dynamic_slice_in_dim(chunk_id, start, Q_BLOCK)
        s = (jnp.einsum('bqhd,bkhd->bhqk', qn, k_nope)
             + jnp.einsum('bqhr,bkr->bhqk', qr, k_r)).astype(jnp.float32) * scale
        mask = chunk_id[None, :] <= cq[:, None]
        s = jnp.where(mask, s, NEG)
        p = jax.nn.softmax(s, axis=-1).astype(v.dtype)
        return jnp.einsum('bhqk,bkhd->bqhd', p, v)

    starts = jnp.arange(L // Q_BLOCK, dtype=jnp.int32) * Q_BLOCK
    out = lax.map(one_block, starts)
    return jnp.moveaxis(out, 0, 1).reshape(B, L, MLA_HEADS * MLA_V)


def fox_mixer(q, k, v, gate, f_logit, b_f, g_q, g_k, pos):
    B, L, _ = q.shape
    q = rms_norm(q.reshape(B, L, FOX_HEADS, FOX_HD), g_q)
    k = rms_norm(k.reshape(B, L, FOX_HEADS, FOX_HD), g_k)
    v = v.reshape(B, L, FOX_HEADS, FOX_HD)
    log_f = jax.nn.log_sigmoid(f_logit.astype(jnp.float32) + b_f.astype(jnp.float32))
    c = jnp.cumsum(log_f, axis=1).transpose(0, 2, 1)
    scale = FOX_HD ** -0.5

    def one_block(start):
        qb = lax.dynamic_slice_in_dim(q, start, Q_BLOCK, axis=1)
        cq = lax.dynamic_slice_in_dim(c, start, Q_BLOCK, axis=2)
        pq = lax.dynamic_slice_in_dim(pos, start, Q_BLOCK)
        s = jnp.einsum('bqhd,bkhd->bhqk', qb, k).astype(jnp.float32) * scale
        s = s + cq[..., :, None] - c[..., None, :]
        mask = pos[None, :] <= pq[:, None]
        s = jnp.where(mask, s, NEG)
        p = jax.nn.softmax(s, axis=-1).astype(v.dtype)
        return jnp.einsum('bhqk,bkhd->bqhd', p, v)

    starts = jnp.arange(L // Q_BLOCK, dtype=jnp.int32) * Q_BLOCK
    out = jnp.moveaxis(lax.map(one_block, starts), 0, 1).reshape(B, L, FOX_W)
    return out * jax.nn.sigmoid(gate)


def conv_ffn(x, w_up, w_conv, b_conv, w_down):
    L = x.shape[1]
    h = x @ w_up
    hp = jnp.pad(h, ((0, 0), (CONV_K - 1, 0), (0, 0)))
    h = b_conv + sum(w_conv[j] * hp[:, j:j + L] for j in range(CONV_K))
    gate, up = jnp.split(h, 2, axis=-1)
    return (jax.nn.gelu(gate, approximate=True) * up) @ w_down


def setup_inputs(seed: int = 0) -> dict:
    key = jax.random.key(seed)
    ks = jax.random.split(key, 20)
    f32 = jnp.float32
    nrm = lambda k, shp, s: jax.random.normal(k, shp, f32) * s
    gain = lambda k, shp: 1.0 + 0.05 * jax.random.normal(k, shp, f32)
    return {
        "x": jax.random.normal(ks[0], (BATCH, SEQ, D_MODEL), f32),
        "meta_tokens": nrm(ks[1], (N_META, D_MODEL), 1.0),
        "ln_mix_pre": gain(ks[2], (DEPTH, D_MODEL)),
        "w_in": nrm(ks[3], (DEPTH, D_MODEL, IN_COLS), D_MODEL ** -0.5),
        "b_forget": FORGET_BIAS + 0.5 * jax.random.normal(ks[4], (DEPTH, FOX_HEADS), f32),
        "g_q_latent": gain(ks[5], (DEPTH, MLA_Q_LORA)),
        "g_kv_latent": gain(ks[6], (DEPTH, MLA_KV_LORA)),
        "w_q_up": nrm(ks[7], (DEPTH, MLA_Q_LORA, MLA_HEADS * (MLA_NOPE + MLA_ROPE)), MLA_Q_LORA ** -0.5),
        "w_kv_up": nrm(ks[8], (DEPTH, MLA_KV_LORA, MLA_HEADS * (MLA_NOPE + MLA_V)), MLA_KV_LORA ** -0.5),
        "g_fox_q": gain(ks[9], (DEPTH, FOX_HD)),
        "g_fox_k": gain(ks[10], (DEPTH, FOX_HD)),
        "w_out": nrm(ks[11], (DEPTH, MIX_WIDTH, D_MODEL), MIX_WIDTH ** -0.5),
        "ln_mix_post": gain(ks[12], (DEPTH, D_MODEL)),
        "ln_ffn_pre": gain(ks[13], (DEPTH, D_MODEL)),
        "w_ffn_up": nrm(ks[14], (DEPTH, D_MODEL, 2 * D_FF), D_MODEL ** -0.5),
        "w_ffn_conv": nrm(ks[15], (DEPTH, CONV_K, 2 * D_FF), CONV_K ** -0.5),
        "b_ffn_conv": nrm(ks[16], (DEPTH, 2 * D_FF), 0.02),
        "w_ffn_down": nrm(ks[17], (DEPTH, D_FF, D_MODEL), D_FF ** -0.5),
        "ln_ffn_post": gain(ks[18], (DEPTH, D_MODEL)),
    }


def reference(x, meta_tokens, ln_mix_pre, w_in, b_forget, g_q_latent, g_kv_latent, w_q_up, w_kv_up,
              g_fox_q, g_fox_k, w_out, ln_mix_post, ln_ffn_pre, w_ffn_up, w_ffn_conv, b_ffn_conv,
              w_ffn_down, ln_ffn_post):
    B, S, D = x.shape
    L = N_META + S
    L_pad = -(-L // Q_BLOCK) * Q_BLOCK
    h = jnp.concatenate([
        jnp.broadcast_to(meta_tokens.astype(x.dtype)[None], (B, N_META, D)),
        x,
        jnp.zeros((B, L_pad - L, D), x.dtype)], axis=1)
    pos = jnp.arange(L_pad, dtype=jnp.int32)
    chunk_id = jnp.where(pos < N_META, 0,
                         jnp.where(pos < L, 1 + (pos - N_META) // CHUNK, 2 + S // CHUNK)).astype(jnp.int32)
    half = MLA_ROPE // 2
    inv_freq = ROPE_THETA ** (-jnp.arange(half, dtype=jnp.float32) / half)
    ang = pos.astype(jnp.float32)[:, None] * inv_freq[None, :]
    cos, sin = jnp.cos(ang), jnp.sin(ang)
    split_idx = [int(v) for v in np.cumsum(IN_SPLIT_SIZES)[:-1]]

    for l in range(DEPTH):
        hn = rms_norm(h, ln_mix_pre[l])
        c_q, c_kv, k_rope, fq, fk, fv, fg, ff = jnp.split(hn @ w_in[l], split_idx, axis=-1)
        a = mla_mixer(c_q, c_kv, k_rope, g_q_latent[l], g_kv_latent[l], w_q_up[l], w_kv_up[l],
                      cos, sin, chunk_id)
        b = fox_mixer(fq, fk, fv, fg, ff, b_forget[l], g_fox_q[l], g_fox_k[l], pos)
        mix = jnp.concatenate([a, b], axis=-1) @ w_out[l]
        h = h + rms_norm(mix, ln_mix_post[l])
        f = conv_ffn(rms_norm(h, ln_ffn_pre[l]), w_ffn_up[l], w_ffn_conv[l], b_ffn_conv[l], w_ffn_down[l])
        h = h + rms_norm(f, ln_ffn_post[l])

    return h[:, N_META:N_META + S]
```

```python
import os
from concourse.bass_utils import run_bass_kernel_spmd


from contextlib import ExitStack
import numpy as np
import concourse.bass as bass
import concourse.mybir as mybir

F32 = mybir.dt.float32
BF16 = mybir.dt.bfloat16
AF = mybir.ActivationFunctionType
ALU = mybir.AluOpType
AX = mybir.AxisListType

ENGS = ("pe", "act", "dve", "pool", "sp")
EPOCH = 12000


class Op:
    __slots__ = ("eng", "fn", "deps", "sig", "is_dma", "slot", "used", "idx")

    def __init__(self, eng, fn, deps, is_dma, slot):
        self.eng = eng
        self.fn = fn
        self.deps = [d for d in deps if d is not None]
        self.sig = None
        self.is_dma = is_dma
        self.slot = slot
        self.used = False


class Prog:
    def __init__(self):
        self.nc = bass.Bass("TRN2", target_bir_lowering=False)
        self.ops = {e: [] for e in ENGS}
        self.ctx = ExitStack()
        self.nsem = 0
        self._names = set()

    def dram(self, name, shape, dt, kind):
        return self.nc.dram_tensor(name, list(shape), dt, kind=kind).ap()

    def sb(self, name, shape, dt):
        return self.ctx.enter_context(self.nc.sbuf_tensor(name, list(shape), dt))

    def ps(self, name, shape, dt=F32):
        return self.ctx.enter_context(self.nc.psum_tensor(name, list(shape), dt))

    def _sem(self, name):
        self.nsem += 1
        return self.ctx.enter_context(self.nc.semaphore(name))

    def op(self, eng, fn, deps=()):
        o = Op(eng, fn, deps, False, None)
        self.ops[eng].append(o)
        return o

    def dma(self, eng, out, in_, deps=(), slot=None, **kw):
        assert slot is not None
        try:
            shp = list(out.shape)
            shp_i = list(in_.shape)
        except Exception:
            shp = shp_i = None
        if shp is not None and len(shp) == 2 and shp == shp_i and shp[1] > 1024:
            last = None
            for c0 in range(0, shp[1], 1024):
                c1 = min(shp[1], c0 + 1024)
                last = self.dma(eng, out[:, c0:c1], in_[:, c0:c1], deps=deps, slot=slot, **kw)
            return last
        o = Op(eng, (lambda e, out=out, in_=in_, kw=kw: e.dma_start(out=out, in_=in_, **kw)), deps, True, slot)
        self.ops[eng].append(o)
        return o

    def wait(self, eng, deps):
        o = Op(eng, None, deps, False, None)
        self.ops[eng].append(o)
        return o

    def build(self):
        nc = self.nc
        for e in ENGS:
            for o in self.ops[e]:
                for d in o.deps:
                    d.used = True
        slot_sems = {}
        for e in ENGS:
            cnt = 0
            sem = None
            for o in self.ops[e]:
                if o.fn is None:
                    continue
                if o.is_dma:
                    if o.slot not in slot_sems:
                        slot_sems[o.slot] = [self._sem("d_%s" % o.slot), 0]
                    ss = slot_sems[o.slot]
                    if ss[1] >= 16 * 1800:
                        ss[0] = self._sem("d_%s_%d" % (o.slot, self.nsem))
                        ss[1] = 0
                    ss[1] += 16
                    o.sig = (ss[0], ss[1], 16)
                elif o.used:
                    if sem is None or cnt >= EPOCH:
                        sem = self._sem("e_%s_%d" % (e, self.nsem))
                        cnt = 0
                    cnt += 1
                    o.sig = (sem, cnt, 1)
        with nc.Block() as block:
            def emit(e, engobj):
                waited = {}
                for o in self.ops[e]:
                    for d in o.deps:
                        sem, val, _ = d.sig
                        k = id(sem)
                        if waited.get(k, 0) < val:
                            engobj.wait_ge(sem, val)
                            waited[k] = val
                    if o.fn is None:
                        continue
                    ins = o.fn(engobj)
                    if o.sig is not None:
                        ins.then_inc(o.sig[0], o.sig[2])

            @block.tensor
            def _(t):
                emit("pe", t)

            @block.scalar
            def _(s):
                emit("act", s)

            @block.vector
            def _(v):
                emit("dve", v)

            @block.gpsimd
            def _(g):
                emit("pool", g)

            @block.sync
            def _(s):
                emit("sp", s)
        self.ctx.close()
        return nc


TP = 1048
CT = [(0, 512), (512, 512), (1024, 24)]
EPS = 1e-6
SC_M = 192.0 ** -0.5
SC_F = 128.0 ** -0.5


class Pools:
    def __init__(self, P, nps=2, nw=3, wsize=16 * 128, banks=3):
        self.P = P
        self.ps = [P.ps("pp%d" % i, [128, banks * 512]) for i in range(nps)]
        self.ps_free = [[] for _ in range(nps)]
        self.ps_i = 0
        self.w = [P.sb("wsl%d" % i, [128, wsize], BF16) for i in range(nw)]
        self.w_free = [[] for _ in range(nw)]
        self.w_i = 0

    def get_ps(self):
        s = self.ps_i % len(self.ps)
        self.ps_i += 1
        d = self.ps_free[s]
        self.ps_free[s] = []
        return s, self.ps[s], d

    def get_w(self):
        s = self.w_i % len(self.w)
        self.w_i += 1
        d = self.w_free[s]
        self.w_free[s] = []
        return s, self.w[s], d


def fm_chunk(P, pools, wsrc, nk, M, src, src_ready, m0=0, wl=None, ws=None, cts=CT, wcols=None):
    if wl is None:
        ws, wt, wd = pools.get_w()
        wc_ = wcols if wcols is not None else nk * 128
        wl = P.dma("pool", wt[:, 0:wc_], wsrc, deps=wd, slot="w%d" % ws)
    wt = pools.w[ws]
    s, pt, pd = pools.get_ps()
    last = None
    first = True
    for ti, (c0, n) in enumerate(cts):
        for kc in range(nk):
            deps = ([wl] + list(src_ready) + pd) if first else []
            first = False
            last = P.op("pe", (lambda e, o=pt[0:M, ti * 512:ti * 512 + n], l=wt[:, kc * 128 + m0:kc * 128 + m0 + M],
                               r=src[:, kc, c0:c0 + n], a=(kc == 0), b=(kc == nk - 1):
                               e.matmul(o, l, r, start=a, stop=b)), deps)
    pools.w_free[ws] = [last]
    return s, pt, last, wl, ws


def ps_view(pt, M, cts=CT):
    return [pt[0:M, ti * 512:ti * 512 + n] for ti, (c0, n) in enumerate(cts)]


def rms_stats(P, pools, srcs, src_ready, onesw, ones_ready, sq_bufs, sq_free, rstd_out, eps_ap, scale=1.0, cts=CT, wdeps=()):
    s, pt, pd = pools.get_ps()
    nsrc = len(srcs)
    last = None
    for i, sap in enumerate(srcs):
        b = i % len(sq_bufs)
        sq = P.op("act", (lambda e, o=sq_bufs[b][:, 0:TP], a=sap: e.activation(out=o, in_=a, func=AF.Square)),
                  list(src_ready) + sq_free[b])
        for ti, (c0, n) in enumerate(cts):
            deps = [sq, ones_ready] + (pd if i == 0 else [])
            last = P.op("pe", (lambda e, o=pt[:, ti * 512:ti * 512 + n], r=sq_bufs[b][:, c0:c0 + n], a=(i == 0), bb=(i == nsrc - 1):
                               e.matmul(o, onesw, r, start=a, stop=bb)), deps)
        sq_free[b] = [last]
    k = 1.0 / (scale * scale)
    ops = []
    for ti, (c0, n) in enumerate(cts):
        o1 = P.op("act", (lambda e, o=rstd_out[:, c0:c0 + n], a=pt[:, ti * 512:ti * 512 + n]:
                          e.activation(out=o, in_=a, func=AF.Sqrt, bias=eps_ap, scale=k)), [last] + list(wdeps))
        ops.append(o1)
    pools.ps_free[s] = list(ops)
    r = P.op("dve", lambda e: e.reciprocal(out=rstd_out[:, 0:TP], in_=rstd_out[:, 0:TP]), ops)
    return r


def build_k1(P=None, io=None):
    own = P is None
    if own:
        P = Prog()
    io = io or {}

    def din(name, shape, dt):
        return io[name] if name in io else P.dram(name, shape, dt, "ExternalInput")

    def dout(name, shape, dt):
        return io[name] if name in io else P.dram(name, shape, dt, "ExternalOutput")

    xT_d = din("xT", [2048, TP], F32)
    w1_d = din("w1", [34, 128, 16 * 128], F32)
    w1v_d = din("w1v", [2, 128, 16 * 512], F32)
    wq_d = din("wq", [16, 128, 4 * 128], F32)
    wk_d = din("wk", [8, 128, 4 * 128], F32)
    wv_d = din("wv", [2, 128, 4 * 512], F32)
    gpre_d = din("g_pre", [128, 16], F32)
    gq_d = din("g_ql", [128, 4], F32)
    gkv_d = din("g_kvl", [128, 4], F32)
    gfq_d = din("g_fq", [128, 1], F32)
    gfk_d = din("g_fk", [128, 1], F32)
    bf_d = din("b_f", [8, 1], F32)
    cos_d = din("cos2", [64, TP], F32)
    sin_d = din("sin2", [64, TP], F32)
    on2048_d = din("c_on2048", [128, 128], BF16)
    on512_d = din("c_on512", [128, 128], BF16)
    on128_d = din("c_on128", [128, 128], BF16)

    QM_d = dout("QM", [8, 192, TP], BF16)
    KN_d = dout("KN", [8, 128, TP], BF16)
    KR_d = dout("KR", [64, TP], BF16)
    VM_d = dout("VM", [TP, 1024], BF16)
    FQ_d = dout("FQ", [8, 128, TP], BF16)
    FK_d = dout("FK", [8, 128, TP], BF16)
    VF_d = dout("VF", [TP, 1024], BF16)
    LOGF_d = dout("LOGF", [8, TP], F32)
    GATE_d = dout("GATE", [8, 128, TP], F32)

    xst = [P.sb("xst%d" % i, [128, TP], F32) for i in range(3)]
    xst_free = [[], [], []]
    hn = P.sb("hn", [128, 16, TP], BF16)
    cq = P.sb("cq", [128, 4, TP], F32)
    ckv = P.sb("ckv", [128, 4, TP], F32)
    cqn = P.sb("cqn", [128, 4, TP], BF16)
    ckvn = P.sb("ckvn", [128, 4, TP], BF16)
    rstd = P.sb("rstd", [128, TP], F32)
    rstd2 = P.sb("rstd2", [128, TP], F32)
    sqb = [P.sb("sqb%d" % i, [128, TP], BF16) for i in range(2)]
    sq_free = [[], []]
    gpre = P.sb("gpre", [128, 16], F32); gq = P.sb("gq", [128, 4], F32); gkv = P.sb("gkv", [128, 4], F32)
    gfq = P.sb("gfq", [128, 1], F32); gfk = P.sb("gfk", [128, 1], F32); bfb = P.sb("bfb", [8, 1], F32)
    cos2 = P.sb("cos2s", [64, TP], F32); sin2 = P.sb("sin2s", [64, TP], F32)
    on2048 = P.sb("on2048", [128, 128], BF16); on512 = P.sb("on512", [128, 128], BF16); on128 = P.sb("on128", [128, 128], BF16)
    epsb = P.sb("epsb", [128, 1], F32); epsf = P.sb("epsf", [128, 1], F32)
    NE = 3
    ev_f = [P.sb("evf%d" % i, [128, TP], F32) for i in range(NE)]
    ev_b = [P.sb("evb%d" % i, [128, TP], BF16) for i in range(NE)]
    evf_free = [[] for _ in range(NE)]
    evb_free = [[] for _ in range(NE)]
    cnt = {"f": 0, "b": 0}
    tmv = [P.sb("tmv%d" % i, [128, 512], BF16) for i in range(2)]
    tmv_free = [[], []]
    wvs = P.sb("wvs", [128, 16 * 512], BF16)
    t1 = P.sb("t1", [64, TP], F32); t2 = P.sb("t2", [64, TP], F32)

    pools = Pools(P)

    def get_evf():
        s = cnt["f"] % NE; cnt["f"] += 1
        d = evf_free[s]; evf_free[s] = []
        return s, ev_f[s], d

    def get_evb():
        s = cnt["b"] % NE; cnt["b"] += 1
        d = evb_free[s]; evb_free[s] = []
        return s, ev_b[s], d

    L = {}
    for nm, s, d in (("gpre", gpre, gpre_d), ("gq", gq, gq_d), ("gkv", gkv, gkv_d), ("gfq", gfq, gfq_d), ("gfk", gfk, gfk_d),
                     ("bf", bfb, bf_d), ("cos", cos2, cos_d), ("sin", sin2, sin_d), ("on2048", on2048, on2048_d),
                     ("on512", on512, on512_d), ("on128", on128, on128_d)):
        L[nm] = P.dma("sp", s[:], d, slot="c_" + nm)
    m1 = P.op("dve", lambda e: e.memset(epsb[:], EPS), [])
    m2 = P.op("dve", lambda e: e.memset(epsf[:], EPS / (SC_F * SC_F)), [])

    xv = xT_d.rearrange("(kc p) t -> p kc t", p=128)
    s0, pt0, pd0 = pools.get_ps()
    last = None
    xi = 0
    for kc in range(16):
        b = xi % 3; xi += 1
        ld = P.dma("sp", xst[b][:, :], xv[:, kc, :], deps=xst_free[b], slot="x%d" % b)
        sb_ = kc % 2
        sq = P.op("act", (lambda e, o=sqb[sb_][:, 0:TP], a=xst[b][:, :]: e.activation(out=o, in_=a, func=AF.Square)),
                  [ld] + sq_free[sb_])
        xst_free[b] = [sq]
        for ti, (c0, n) in enumerate(CT):
            last = P.op("pe", (lambda e, o=pt0[:, ti * 512:ti * 512 + n], r=sqb[sb_][:, c0:c0 + n], a=(kc == 0), bb=(kc == 15):
                               e.matmul(o, on2048[:, :], r, start=a, stop=bb)), [sq, L["on2048"]] + (pd0 if kc == 0 else []))
        sq_free[sb_] = [last]
    sops = []
    for ti, (c0, n) in enumerate(CT):
        sops.append(P.op("act", (lambda e, o=rstd[:, c0:c0 + n], a=pt0[:, ti * 512:ti * 512 + n]:
                                 e.activation(out=o, in_=a, func=AF.Sqrt, bias=epsb[:, 0:1], scale=1.0)), [last, m1]))
    pools.ps_free[s0] = list(sops)
    r0 = P.op("dve", lambda e: e.reciprocal(out=rstd[:, 0:TP], in_=rstd[:, 0:TP]), sops)
    hn_ops = []
    for kc in range(16):
        b = xi % 3; xi += 1
        ld = P.dma("sp", xst[b][:, :], xv[:, kc, :], deps=xst_free[b], slot="x%d" % b)
        o_ = P.op("dve", (lambda e, kc=kc, b=b: e.scalar_tensor_tensor(
            out=hn[:, kc, :], in0=xst[b][:, :], scalar=gpre[:, kc:kc + 1], in1=rstd[:, :], op0=ALU.mult, op1=ALU.mult)),
            [r0, L["gpre"], ld])
        xst_free[b] = [o_]
        hn_ops.append(o_)
    hn_ready = [hn_ops[-1]]
    STOP = int(os.environ.get('K1_STOP', '99'))
    if STOP <= 1:
        P.wait('sp', hn_ops[-1:]); return P.build()

    def out_dma(dst, src, dep, buf_kind, slot_i):
        o = P.dma("sp", dst, src, deps=[dep], slot="o%s%d" % (buf_kind, slot_i))
        if buf_kind == "b":
            evb_free[slot_i] = [o]
        else:
            evf_free[slot_i] = [o]
        return o

    final = []

    for which, dst in ((0, cq), (1, ckv)):
        for j in range(4):
            s, pt, mm, _, _ = fm_chunk(P, pools, w1_d[which * 4 + j], 16, 128, hn, hn_ready)
            evs = []
            for ti, (c0, n) in enumerate(CT):
                evs.append(P.op("act", (lambda e, o=dst[:, j, c0:c0 + n], a=pt[:, ti * 512:ti * 512 + n]:
                                        e.activation(out=o, in_=a, func=AF.Copy)), [mm]))
            pools.ps_free[s] = evs
            if which == 0 and j == 3:
                cq_ready = evs
            if which == 1 and j == 3:
                ckv_ready = evs
    if STOP <= 2:
        P.wait('sp', cq_ready + ckv_ready); return P.build()
    s1, pt1, mm1, wl, ws = fm_chunk(P, pools, w1_d[8], 16, 64, hn, hn_ready, m0=0)
    s2, pt2, mm2, _, _ = fm_chunk(P, pools, None, 16, 64, hn, hn_ready, m0=64, wl=wl, ws=ws)

    def rope_combine(pt1, s1, mm1, pt2, s2, mm2, dst_dram, scale):
        a = P.op("dve", lambda e: e.tensor_tensor(out=t1[:, :], in0=pt1_v(pt1), in1=cos2[:, :], op=ALU.mult), [mm1, L["cos"]])
        return a

    def rope(pt1, s1, mm1, pt2, s2, mm2, dst_dram, scale, extra_free):
        o_a = []
        o_b = []
        for ti, (c0, n) in enumerate(CT):
            o_a.append(P.op("dve", (lambda e, c0=c0, n=n, ti=ti: e.tensor_tensor(
                out=t1[:, c0:c0 + n], in0=pt1[0:64, ti * 512:ti * 512 + n], in1=cos2[:, c0:c0 + n], op=ALU.mult)),
                [mm1, L["cos"]] + extra_free))
            o_b.append(P.op("dve", (lambda e, c0=c0, n=n, ti=ti: e.tensor_tensor(
                out=t2[:, c0:c0 + n], in0=pt2[0:64, ti * 512:ti * 512 + n], in1=sin2[:, c0:c0 + n], op=ALU.mult)),
                [mm2, L["sin"]]))
        pools.ps_free[s1] = o_a
        pools.ps_free[s2] = o_b
        bs, bt, bd = get_evb()
        oc = P.op("dve", lambda e: e.scalar_tensor_tensor(out=bt[0:64, :], in0=t1[:, :], scalar=scale, in1=t2[:, :],
                                                          op0=ALU.mult, op1=ALU.add), o_a + o_b + bd)
        return bs, bt, oc

    def rope2(pt1, s1, mm1, pt2, s2, mm2, scale, extra_free):
        o_a = []
        o_b = []
        for ti, (c0, n) in enumerate(CT):
            o_a.append(P.op("dve", (lambda e, c0=c0, n=n, ti=ti: e.tensor_tensor(
                out=t1[:, c0:c0 + n], in0=pt1[0:64, ti * 512:ti * 512 + n], in1=cos2[:, c0:c0 + n], op=ALU.mult)),
                [mm1, L["cos"]] + extra_free))
        pools.ps_free[s1] = o_a
        for ti, (c0, n) in enumerate(CT):
            o_b.append(P.op("dve", (lambda e, c0=c0, n=n, ti=ti: e.tensor_tensor(
                out=t2[:, c0:c0 + n], in0=pt2[0:64, ti * 512:ti * 512 + n], in1=sin2[:, c0:c0 + n], op=ALU.mult)),
                [mm2, L["sin"]] + o_a))
        pools.ps_free[s2] = o_b
        o_c = P.op("dve", lambda e: e.tensor_tensor(out=t1[:, :], in0=t1[:, :], in1=t2[:, :], op=ALU.add), o_b)
        bs, bt, bd = get_evb()
        o_d = P.op("act", lambda e: e.activation(out=bt[0:64, :], in_=t1[:, :], func=AF.Identity, scale=scale), [o_c] + bd)
        return bs, bt, o_d

    bs, bt, od = rope2(pt1, s1, mm1, pt2, s2, mm2, 1.0, [])
    rope_prev = [od]
    final.append(out_dma(KR_d[:, :], bt[0:64, :], od, "b", bs))

    if STOP <= 3:
        P.wait('sp', final); return P.build()
    rstd2_rd = []
    for which, dst_d, gain, gl, sc, eps_ap in ((0, FQ_d, gfq, "gfq", SC_F, epsf), (1, FK_d, gfk, "gfk", 1.0, epsb)):
        for h in range(8):
            s, pt, mm, _, _ = fm_chunk(P, pools, w1_d[9 + which * 8 + h], 16, 128, hn, hn_ready)
            fs, ft, fd = get_evf()
            evs = []
            for ti, (c0, n) in enumerate(CT):
                evs.append(P.op("act", (lambda e, o=ft[:, c0:c0 + n], a=pt[:, ti * 512:ti * 512 + n]:
                                        e.activation(out=o, in_=a, func=AF.Copy)), [mm] + (fd if ti == 0 else [])))
            pools.ps_free[s] = evs
            rr = rms_stats(P, pools, [ft[:, :]], evs, on128[:, :], L["on128"], sqb, sq_free, rstd2, eps_ap[:, 0:1], scale=sc, wdeps=[m1, m2] + rstd2_rd)
            bs, bt, bd = get_evb()
            on_ = P.op("dve", (lambda e, bt=bt, ft=ft, gain=gain: e.scalar_tensor_tensor(
                out=bt[:, :], in0=ft[:, :], scalar=gain[:, 0:1], in1=rstd2[:, :], op0=ALU.mult, op1=ALU.mult)),
                [rr, L[gl], m2] + bd)
            evf_free[fs] = [on_]
            rstd2_rd = [on_]
            final.append(out_dma(dst_d[h], bt[:, :], on_, "b", bs))
    if STOP <= 4:
        P.wait('sp', final); return P.build()
    for h in range(8):
        s, pt, mm, _, _ = fm_chunk(P, pools, w1_d[25 + h], 16, 128, hn, hn_ready)
        fs, ft, fd = get_evf()
        evs = []
        for ti, (c0, n) in enumerate(CT):
            evs.append(P.op("act", (lambda e, o=ft[:, c0:c0 + n], a=pt[:, ti * 512:ti * 512 + n]:
                                    e.activation(out=o, in_=a, func=AF.Sigmoid)), [mm] + (fd if ti == 0 else [])))
        pools.ps_free[s] = evs
        final.append(out_dma(GATE_d[h], ft[:, :], evs[-1], "f", fs))
        evf_free[fs] = [final[-1]]
    s, pt, mm, _, _ = fm_chunk(P, pools, w1_d[33], 16, 8, hn, hn_ready)
    fs, ft, fd = get_evf()
    evs = []
    for ti, (c0, n) in enumerate(CT):
        evs.append(P.op("act", (lambda e, o=ft[0:8, c0:c0 + n], a=pt[0:8, ti * 512:ti * 512 + n]:
                                e.activation(out=o, in_=a, func=AF.Sigmoid, bias=bfb[:, 0:1], scale=1.0)),
                        [mm, L["bf"]] + (fd if ti == 0 else [])))
    pools.ps_free[s] = evs
    lg = P.op("act", lambda e: e.activation(out=ft[0:8, :], in_=ft[0:8, :], func=AF.Ln), evs)
    final.append(out_dma(LOGF_d[:, :], ft[0:8, :], lg, "f", fs))

    if STOP <= 5:
        P.wait('sp', final); return P.build()
    TT = [(128 * i, 128) for i in range(8)] + [(1024, 24)]

    def tm_group(wsrc, nk, src, src_ready, dst_d, col0, wfree):
        wl = P.dma("pool", wvs[:, 0:nk * 512], wsrc, deps=wfree, slot="wv")
        last = None
        for ti, (c0, n) in enumerate(TT):
            s, pt, pd = pools.get_ps()
            for kc in range(nk):
                deps = ([wl] + list(src_ready) + pd) if kc == 0 else []
                last = P.op("pe", (lambda e, o=pt[0:n, 0:512], l=src[:, kc, c0:c0 + n], r=wvs[:, kc * 512:(kc + 1) * 512],
                                   a=(kc == 0), b=(kc == nk - 1): e.matmul(o, l, r, start=a, stop=b)), deps)
            b = ti % 2
            ev = P.op("act", (lambda e, o=tmv[b][0:n, :], a=pt[0:n, 0:512]: e.activation(out=o, in_=a, func=AF.Copy)),
                      [last] + tmv_free[b])
            pools.ps_free[s] = [ev]
            od = P.dma("sp", dst_d[c0:c0 + n, col0:col0 + 512], tmv[b][0:n, :], deps=[ev], slot="otm%d" % b)
            tmv_free[b] = [od]
            final.append(od)
        return [last]

    wfree = []
    for g in range(2):
        wfree = tm_group(w1v_d[g], 16, hn, hn_ready, VF_d, 512 * g, wfree)

    if STOP <= 6:
        P.wait('sp', final); return P.build()
    rq = rms_stats(P, pools, [cq[:, j, :] for j in range(4)], cq_ready, on512[:, :], L["on512"], sqb, sq_free, rstd, epsb[:, 0:1], wdeps=hn_ops[-1:])
    qn_ops = [P.op("dve", (lambda e, j=j: e.scalar_tensor_tensor(out=cqn[:, j, :], in0=cq[:, j, :], scalar=gq[:, j:j + 1],
                                                                  in1=rstd[:, :], op0=ALU.mult, op1=ALU.mult)),
                   [rq, L["gq"]]) for j in range(4)]
    rk = rms_stats(P, pools, [ckv[:, j, :] for j in range(4)], ckv_ready, on512[:, :], L["on512"], sqb, sq_free, rstd2, epsb[:, 0:1], wdeps=rstd2_rd)
    kn_ops = [P.op("dve", (lambda e, j=j: e.scalar_tensor_tensor(out=ckvn[:, j, :], in0=ckv[:, j, :], scalar=gkv[:, j:j + 1],
                                                                  in1=rstd2[:, :], op0=ALU.mult, op1=ALU.mult)),
                   [rk, L["gkv"]]) for j in range(4)]
    cqn_ready = [qn_ops[-1]]
    ckvn_ready = [kn_ops[-1]]
    for h in range(8):
        s, pt, mm, _, _ = fm_chunk(P, pools, wq_d[2 * h], 4, 128, cqn, cqn_ready)
        bs, bt, bd = get_evb()
        evs = []
        for ti, (c0, n) in enumerate(CT):
            evs.append(P.op("act", (lambda e, o=bt[:, c0:c0 + n], a=pt[:, ti * 512:ti * 512 + n]:
                                    e.activation(out=o, in_=a, func=AF.Identity, scale=SC_M)), [mm] + (bd if ti == 0 else [])))
        pools.ps_free[s] = evs
        final.append(out_dma(QM_d[h, 0:128, :], bt[:, :], evs[-1], "b", bs))
        s1, pt1, mm1, wl, ws = fm_chunk(P, pools, wq_d[2 * h + 1], 4, 64, cqn, cqn_ready, m0=0)
        s2, pt2, mm2, _, _ = fm_chunk(P, pools, None, 4, 64, cqn, cqn_ready, m0=64, wl=wl, ws=ws)
        bs, bt, od = rope2(pt1, s1, mm1, pt2, s2, mm2, SC_M, rope_prev)
        rope_prev = [od]
        final.append(out_dma(QM_d[h, 128:192, :], bt[0:64, :], od, "b", bs))
    for h in range(8):
        s, pt, mm, _, _ = fm_chunk(P, pools, wk_d[h], 4, 128, ckvn, ckvn_ready)
        bs, bt, bd = get_evb()
        evs = []
        for ti, (c0, n) in enumerate(CT):
            evs.append(P.op("act", (lambda e, o=bt[:, c0:c0 + n], a=pt[:, ti * 512:ti * 512 + n]:
                                    e.activation(out=o, in_=a, func=AF.Copy)), [mm] + (bd if ti == 0 else [])))
        pools.ps_free[s] = evs
        final.append(out_dma(KN_d[h], bt[:, :], evs[-1], "b", bs))
    for g in range(2):
        wfree = tm_group(wv_d[g], 4, ckvn, ckvn_ready, VM_d, 512 * g, wfree)

    P.wait("sp", final)
    if own:
        return P.build()
    return final


NQB = 16
NB = 4 * NQB
TG = 512 * NQB + 16
FR = 512 * NQB
NBB = NB + 1
MASKV = -30000.0


def k2_consts():
    import ml_dtypes
    bf = ml_dtypes.bfloat16
    p = np.arange(128)[:, None]
    col = np.arange(512)[None, :]
    mf = np.zeros((128, 4, 512), np.float32)
    mm = np.zeros((128, 4, 512), np.float32)
    for i in range(4):
        k = 128 * i + p
        mf[:, i, :] = np.where(k <= col, 0.0, MASKV)
        mm[:, i, :] = np.where((k // 64) <= (col // 64), 0.0, MASKV)
    mmeta = np.where(np.arange(16)[:, None] <= np.arange(16)[None, :], 0.0, MASKV).astype(np.float32)
    tri = (np.arange(128)[:, None] <= np.arange(128)[None, :]).astype(np.float32)
    st = (np.arange(NBB)[:, None] < np.arange(NBB)[None, :]).astype(np.float32)
    return {
        "c_maskf": mf.reshape(128, 2048).astype(bf),
        "c_maskm": mm.reshape(128, 2048).astype(bf),
        "c_maskmeta": mmeta.astype(bf),
        "c_tri": tri,
        "c_st": st,
        "c_identf": np.eye(128, dtype=np.float32),
        "c_identb": np.eye(128, dtype=np.float32).astype(bf),
        "c_onesb": np.ones((128, 128), np.float32).astype(bf),
        "c_onesf": np.ones((128, 128), np.float32),
    }


def build_k2(P=None, io=None):
    own = P is None
    if own:
        P = Prog()
    nc = P.nc
    D = {}

    def din(name, shape, dt):
        if io is not None and name in io:
            return io[name]
        return P.dram(name, shape, dt, "ExternalInput")

    qn_d = din("qn", [128, TG], BF16); qr_d = din("qr", [64, TG], BF16)
    kn_d = din("kn", [128, TG], BF16); kr_d = din("kr", [64, TG], BF16)
    vm_d = din("vm", [128, NBB * 128], BF16)
    fq_d = din("fq", [128, TG], BF16); fk_d = din("fk", [128, TG], BF16)
    vf_d = din("vf", [128, NBB * 128], BF16)
    lf_d = din("lf", [128, NBB], F32)
    cmf_d = din("c_maskf", [128, 2048], BF16); cmm_d = din("c_maskm", [128, 2048], BF16)
    cmeta_d = din("c_maskmeta", [16, 16], BF16)
    tri_d = din("c_tri", [128, 128], F32); st_d = din("c_st", [NBB, NBB], F32)
    idf_d = din("c_identf", [128, 128], F32); idb_d = din("c_identb", [128, 128], BF16)
    onb_d = din("c_onesb", [128, 128], BF16); onf_d = din("c_onesf", [128, 128], F32)
    if io is not None and "om" in io:
        om_d, of_d = io["om"], io["of"]
    else:
        om_d = P.dram("om", [128, TG], F32, "ExternalOutput")
        of_d = P.dram("of", [128, TG], F32, "ExternalOutput")

    qn = P.sb("qn_s", [128, TG], BF16); qr = P.sb("qr_s", [64, TG], BF16)
    kn = P.sb("kn_s", [128, TG], BF16); kr = P.sb("kr_s", [64, TG], BF16)
    vm = P.sb("vm_s", [128, NBB * 128], BF16)
    fq = P.sb("fq_s", [128, TG], BF16); fk = P.sb("fk_s", [128, TG], BF16)
    vf = P.sb("vf_s", [128, NBB * 128], BF16)
    lf = P.sb("lf_s", [128, NBB], F32)
    cmf = P.sb("cmf_s", [128, 2048], BF16); cmm = P.sb("cmm_s", [128, 2048], BF16)
    cmeta = P.sb("cmeta_s", [16, 16], BF16)
    tri = P.sb("tri_s", [128, 128], F32); st = P.sb("st_s", [NBB, NBB], F32)
    idf = P.sb("idf_s", [128, 128], F32); idb = P.sb("idb_s", [128, 128], BF16)
    onb = P.sb("onb_s", [128, 128], BF16); onf = P.sb("onf_s", [128, 128], F32)
    xs = P.sb("xs_s", [NBB, 128], F32)
    c_sb = P.sb("c_sb", [128, NBB], F32)
    negc = P.sb("negc", [128, NBB], F32)
    ct = P.sb("ct", [NBB, 128], F32)
    chi = P.sb("chi", [NBB, 128], BF16); cmi = P.sb("cmi", [NBB, 128], BF16); clo = P.sb("clo", [NBB, 128], BF16)
    r1 = P.sb("r1", [NBB, 128], F32); r2 = P.sb("r2", [NBB, 128], F32)
    crow = P.sb("crow", [3, NBB * 128], BF16)
    NS = 3
    p_sb = [P.sb("p_sb%d" % i, [128, 512], BF16) for i in range(NS)]
    o_sb = [P.sb("o_sb%d" % i, [128, 512], F32) for i in range(2)]
    rden = [P.sb("rden%d" % i, [128, 512], F32) for i in range(2)]
    s_ps = [P.ps("s_ps%d" % i, [128, 512]) for i in range(NS)]
    o_ps = [P.ps("o_ps%d" % i, [128, 512]) for i in range(2)]
    d_ps = [P.ps("d_ps%d" % i, [128, 512]) for i in range(2)]

    L = {}
    L["lf"] = P.dma("sp", lf[:], lf_d, slot="lf")
    for nm, s, d in (("tri", tri, tri_d), ("st", st, st_d), ("idf", idf, idf_d), ("onf", onf, onf_d),
                     ("idb", idb, idb_d), ("onb", onb, onb_d), ("cmeta", cmeta, cmeta_d),
                     ("cmf", cmf, cmf_d), ("cmm", cmm, cmm_d)):
        L[nm] = P.dma("act", s[:], d, slot="c_" + nm)
    for nm, s, d in (("qn", qn, qn_d), ("kn", kn, kn_d), ("qr", qr, qr_d), ("kr", kr, kr_d), ("vm", vm, vm_d),
                     ("fq", fq, fq_d), ("fk", fk, fk_d), ("vf", vf, vf_d)):
        ncol = s.shape[1]
        for c0 in range(0, ncol, 1024):
            c1 = min(ncol, c0 + 1024)
            L[nm] = P.dma("sp", s[:, c0:c1], d[:, c0:c1], slot="in_" + nm)

    TYPES = 'mf'
    prolog_free = {0: None, 1: None, 2: None}
    crow_ready = None
    o6 = None
    if 'f' in TYPES:
        x_ps = s_ps[0]
        o1 = P.op("pe", lambda e: e.matmul(x_ps[0:NBB, 0:128], lf[:, :], onf[:, :], start=True, stop=True),
                  [L["lf"], L["onf"]])
        o2 = P.op("dve", lambda e: e.tensor_copy(out=xs[:, :], in_=x_ps[0:NBB, 0:128]), [o1])
        c_ps = s_ps[1]
        o3 = P.op("pe", lambda e: e.matmul(c_ps[:, 0:NBB], tri[:, :], lf[:, :], start=True, stop=False), [L["tri"], L["lf"]])
        o4 = P.op("pe", lambda e: e.matmul(c_ps[:, 0:NBB], xs[:, :], st[:, :], start=False, stop=True), [o2, L["st"], o3])
        o5 = P.op("dve", lambda e: e.tensor_copy(out=c_sb[:, :], in_=c_ps[:, 0:NBB]), [o4])
        o6 = P.op("dve", lambda e: e.tensor_scalar(out=negc[:, :], in0=c_sb[:, :], scalar1=-1.0, scalar2=None, op0=ALU.mult), [o5])
        t_ps = s_ps[2]
        o7 = P.op("pe", lambda e: e.transpose(t_ps[0:NBB, 0:128], c_sb[:, :], idf[:, :]), [o5, L["idf"]])
        o8 = P.op("dve", lambda e: e.tensor_copy(out=ct[:, :], in_=t_ps[0:NBB, 0:128]), [o7])
        o9 = P.op("dve", lambda e: e.tensor_copy(out=chi[:, :], in_=ct[:, :]), [o8])
        o10 = P.op("dve", lambda e: e.tensor_tensor(out=r1[:, :], in0=ct[:, :], in1=chi[:, :], op=ALU.subtract), [o9])
        o11 = P.op("dve", lambda e: e.tensor_copy(out=cmi[:, :], in_=r1[:, :]), [o10])
        o12 = P.op("dve", lambda e: e.tensor_tensor(out=r2[:, :], in0=r1[:, :], in1=cmi[:, :], op=ALU.subtract), [o11])
        o13 = P.op("dve", lambda e: e.tensor_copy(out=clo[:, :], in_=r2[:, :]), [o12])
        cscr = P.dram("cscr", [3, NBB * 128], BF16, "Internal")
        cw = []
        for j, (src, dep) in enumerate(((chi, o9), (cmi, o11), (clo, o13))):
            cw.append(P.dma("sp", cscr[j].rearrange("(b t) -> b t", t=128), src[:, :], deps=[dep], slot="cscr"))
        crow_ready = P.dma("sp", crow[0:3, :], cscr[:, :], deps=[cw[-1]], slot="crow")
        pre_done = [o8]
        prolog_free = {0: o2, 1: o5, 2: o8}


    pairs = []
    for typ in 'mf':
        pairs.append(dict(typ=typ, qb=-1, q0=FR, n=16, kb=-1, first=True, last=True, diag=None))
        for j in range(NQB):
            kbs = [-1] + list(range(4 * j + 4))
            for ii, kb in enumerate(kbs):
                pairs.append(dict(typ=typ, qb=j, q0=512 * j, n=512, kb=kb, first=(ii == 0), last=(ii == len(kbs) - 1),
                                  diag=(kb - 4 * j) if kb >= 4 * j else None))
    npairs = len(pairs)

    s_done = [None] * npairs
    e_done = [None] * npairs
    pv_done = [None] * npairs
    norm_done = {}
    out_dma = {}
    qseq = -1
    qseq_of = [0] * npairs
    for i, pr in enumerate(pairs):
        if pr["first"]:
            qseq += 1
        qseq_of[i] = qseq

    def emit_S(i):
        pr = pairs[i]
        typ, q0, n, kb = pr["typ"], pr["q0"], pr["n"], pr["kb"]
        sl = i % NS
        rows = 16 if kb < 0 else 128
        k0 = FR if kb < 0 else 128 * kb
        sp_ = s_ps[sl]
        deps = []
        if i - NS >= 0:
            deps.append(e_done[i - NS])
        else:
            deps.append(prolog_free[sl])
        out = sp_[0:rows, 0:n]
        steps = []
        if typ == "m":
            steps.append((kn[:, k0:k0 + rows], qn[:, q0:q0 + n], [L["kn"], L["qn"]]))
            steps.append((kr[:, k0:k0 + rows], qr[:, q0:q0 + n], [L["kr"], L["qr"]]))
        else:
            steps.append((fk[:, k0:k0 + rows], fq[:, q0:q0 + n], [L["fk"], L["fq"]]))
            cq0 = 0 if pr["qb"] < 0 else 128 + 512 * pr["qb"]
            steps.append((onb[0:3, 0:rows], crow[0:3, cq0:cq0 + n], [L["onb"], crow_ready]))
        if pr["qb"] < 0:
            if typ == "f":
                steps.append((idb[0:16, 0:16], cmeta[0:16, 0:16], [L["idb"], L["cmeta"]]))
        elif pr["diag"] is not None:
            mk = cmf if typ == "f" else cmm
            dg = pr["diag"]
            steps.append((idb[:, :], mk[:, 512 * dg:512 * dg + 512], [L["idb"], L["cmf"], L["cmm"]]))
        last = None
        for si, (lt, rh, dd) in enumerate(steps):
            last = P.op("pe", (lambda e, out=out, lt=lt, rh=rh, a=(si == 0), b=(si == len(steps) - 1):
                               e.matmul(out, lt, rh, start=a, stop=b)), deps + dd if si == 0 else dd)
        s_done[i] = last

    def emit_E(i):
        pr = pairs[i]
        typ, n, kb = pr["typ"], pr["n"], pr["kb"]
        sl = i % NS
        rows = 16 if kb < 0 else 128
        deps = [s_done[i]]
        if i - NS >= 0:
            deps.append(pv_done[i - NS])
        if typ == "f":
            cb = 0 if kb < 0 else kb + 1
            bias = negc[0:rows, cb:cb + 1]
            deps.append(o6)
            fn = lambda e, sl=sl, rows=rows, n=n, bias=bias: e.activation(
                out=p_sb[sl][0:rows, 0:n], in_=s_ps[sl][0:rows, 0:n], func=AF.Exp, bias=bias, scale=1.0)
        else:
            fn = lambda e, sl=sl, rows=rows, n=n: e.activation(
                out=p_sb[sl][0:rows, 0:n], in_=s_ps[sl][0:rows, 0:n], func=AF.Exp)
        e_done[i] = P.op("act", fn, deps)

    def emit_PV(i):
        pr = pairs[i]
        typ, q0, n, kb = pr["typ"], pr["q0"], pr["n"], pr["kb"]
        sl = i % NS
        rows = 16 if kb < 0 else 128
        vb = NB if kb < 0 else kb
        v = vm if typ == "m" else vf
        qs = qseq_of[i]
        ob = qs % 2
        deps = [e_done[i], L["vm"], L["vf"], L["onb"]]
        if pr["first"] and qs - 2 in norm_done:
            deps.append(norm_done[qs - 2])
        a, b = pr["first"], pr["last"]
        P.op("pe", lambda e: e.matmul(o_ps[ob][:, 0:n], v[0:rows, vb * 128:vb * 128 + 128], p_sb[sl][0:rows, 0:n],
                                      start=a, stop=b), deps)
        pv_done[i] = P.op("pe", lambda e: e.matmul(d_ps[ob][:, 0:n], onb[0:rows, 0:128], p_sb[sl][0:rows, 0:n],
                                                   start=a, stop=b), [])
        if pr["last"]:
            deps2 = [pv_done[i]]
            if qs - 2 in out_dma:
                deps2.append(out_dma[qs - 2])
            r = P.op("dve", lambda e: e.reciprocal(out=rden[ob][:, 0:n], in_=d_ps[ob][:, 0:n]), deps2)
            nd = P.op("dve", lambda e: e.tensor_tensor(out=o_sb[ob][:, 0:n], in0=o_ps[ob][:, 0:n], in1=rden[ob][:, 0:n],
                                                       op=ALU.mult), [r])
            norm_done[qs] = nd
            dst = om_d if typ == "m" else of_d
            out_dma[qs] = P.dma("sp", dst[:, q0:q0 + n], o_sb[ob][:, 0:n], deps=[nd], slot="out%d" % ob)

    for i in range(npairs + 1):
        if i < npairs:
            emit_S(i)
            emit_E(i)
        if i >= 1:
            emit_PV(i - 1)
    fin = [out_dma[k] for k in sorted(out_dma)[-2:]]
    P.wait("sp", fin)
    if own:
        return P.build()
    return fin


def build_k3a(P=None, io=None):
    own = P is None
    if own:
        P = Prog()
    io = io or {}
    din = lambda name, shape, dt: io[name] if name in io else P.dram(name, shape, dt, "ExternalInput")
    xT_d = din("xT", [2048, TP], F32)
    OM_d = din("OM", [8, 128, TP], F32)
    OF_d = din("OF", [8, 128, TP], F32)
    GATE_d = din("GATE", [8, 128, TP], F32)
    wo_d = din("wo", [16, 128, 16 * 128], F32)
    g_d = din("g_post", [128, 16], F32)
    on_d = din("c_on2048", [128, 128], BF16)
    h1_d = io["h1T"] if "h1T" in io else P.dram("h1T", [2048, TP], F32, "ExternalOutput")

    mixin = P.sb("mixin", [128, 16, TP], BF16)
    mixT = P.sb("mixT", [128, 16, TP], F32)
    stf = [P.sb("stf%d" % i, [128, TP], F32) for i in range(4)]
    st_free = [[] for _ in range(4)]
    rstd = P.sb("rstd", [128, TP], F32)
    sqb = [P.sb("sqb%d" % i, [128, TP], BF16) for i in range(2)]
    sq_free = [[], []]
    gp = P.sb("gp", [128, 16], F32)
    on2048 = P.sb("on2048", [128, 128], BF16)
    epsb = P.sb("epsb", [128, 1], F32)
    pools = Pools(P)
    L = {}
    L["g"] = P.dma("sp", gp[:], g_d, slot="c_g")
    L["on"] = P.dma("sp", on2048[:], on_d, slot="c_on")
    m1 = P.op("dve", lambda e: e.memset(epsb[:], EPS), [])
    mi_ops = []
    for h in range(8):
        mi_ops.append(P.dma("pool", mixin[:, h, :], OM_d[h], slot="om"))
    si = 0
    for h in range(8):
        a = si % 4; b = (si + 1) % 4; si += 2
        la = P.dma("sp", stf[a][:, :], OF_d[h], deps=st_free[a], slot="st%d" % a)
        lb = P.dma("sp", stf[b][:, :], GATE_d[h], deps=st_free[b], slot="st%d" % b)
        o = P.op("dve", (lambda e, h=h, a=a, b=b: e.tensor_tensor(out=mixin[:, 8 + h, :], in0=stf[a][:, :], in1=stf[b][:, :],
                                                                  op=ALU.mult)), [la, lb])
        st_free[a] = [o]; st_free[b] = [o]
        mi_ops.append(o)
    mix_ready = [mi_ops[7], mi_ops[-1]]
    cp_ops = []
    for oc in range(16):
        s, pt, mm, _, _ = fm_chunk(P, pools, wo_d[oc], 16, 128, mixin, mix_ready)
        evs = []
        for ti, (c0, n) in enumerate(CT):
            evs.append(P.op("act", (lambda e, o=mixT[:, oc, c0:c0 + n], a=pt[:, ti * 512:ti * 512 + n]:
                                    e.activation(out=o, in_=a, func=AF.Copy)), [mm]))
        pools.ps_free[s] = evs
        cp_ops.append(evs[-1])
    rr = rms_stats(P, pools, [mixT[:, oc, :] for oc in range(16)], cp_ops[-1:], on2048[:, :], L["on"], sqb, sq_free, rstd,
                   epsb[:, 0:1], wdeps=[m1])
    xv = xT_d.rearrange("(kc p) t -> p kc t", p=128)
    hv = h1_d.rearrange("(kc p) t -> p kc t", p=128)
    fin = []
    for oc in range(16):
        a = si % 4; si += 1
        ld = P.dma("sp", stf[a][:, :], xv[:, oc, :], deps=st_free[a], slot="st%d" % a)
        o1 = P.op("dve", (lambda e, oc=oc: e.scalar_tensor_tensor(out=mixT[:, oc, :], in0=mixT[:, oc, :], scalar=gp[:, oc:oc + 1],
                                                                   in1=rstd[:, :], op0=ALU.mult, op1=ALU.mult)), [rr, L["g"]])
        o2 = P.op("dve", (lambda e, oc=oc, a=a: e.tensor_tensor(out=stf[a][:, :], in0=stf[a][:, :], in1=mixT[:, oc, :], op=ALU.add)),
                  [o1, ld])
        od = P.dma("sp", hv[:, oc, :], stf[a][:, :], deps=[o2], slot="ho%d" % a)
        st_free[a] = [od]
        fin.append(od)
    P.wait("sp", fin)
    if own:
        return P.build()
    return fin


HA = (0, 524)
HB = (522, 526)


def build_k3b(P=None, io=None):
    own = P is None
    if own:
        P = Prog()
    io = io or {}
    din = lambda name, shape, dt: io[name] if name in io else P.dram(name, shape, dt, "ExternalInput")
    h1_d = din("h1T", [2048, TP], F32)
    wu_d = din("wu", [44, 128, 16 * 256], F32)
    wd_d = din("wd", [16, 128, 44 * 128], F32)
    cw_d = din("cw", [128, 44 * 2 * 3], F32)
    cb_d = din("cb", [128, 44 * 2], F32)
    gpre_d = din("g_fpre", [128, 16], F32)
    gpost_d = din("g_fpost", [128, 16], F32)
    on_d = din("c_on2048", [128, 128], BF16)
    h2_d = io["h2T"] if "h2T" in io else P.dram("h2T", [2048, TP], F32, "ExternalOutput")

    hn2 = P.sb("hn2", [128, 16, TP], BF16)
    act = P.sb("actT", [128, 44, 536], BF16)
    fT = P.sb("fT", [128, 16, TP], F32)
    stf = [P.sb("stf%d" % i, [128, TP], F32) for i in range(2)]
    st_free = [[] for _ in range(2)]
    rstd = P.sb("rstd", [128, TP], F32)
    sqb = [P.sb("sqb%d" % i, [128, TP], BF16) for i in range(2)]
    sq_free = [[], []]
    gpre = P.sb("gpre", [128, 16], F32); gpost = P.sb("gpost", [128, 16], F32)
    cw = P.sb("cw_s", [128, 44 * 6], F32); cb = P.sb("cb_s", [128, 44 * 2], F32)
    on2048 = P.sb("on2048", [128, 128], BF16)
    epsb = P.sb("epsb", [128, 1], F32)
    raw = [P.sb("raw%d" % i, [128, 540], F32) for i in range(2)]
    tmp = [P.sb("tmp%d" % i, [128, 536], F32) for i in range(2)]
    edge = P.sb("edge", [128, 88, 2], F32)
    wds = [P.sb("wds%d" % i, [128, 22 * 128], BF16) for i in range(3)]
    wd_free = [[], [], []]
    wdi = 0
    pools = Pools(P, nps=4, nw=2, wsize=16 * 256, banks=2)
    L = {}
    for nm, s, d in (("gpre", gpre, gpre_d), ("gpost", gpost, gpost_d), ("cw", cw, cw_d), ("cb", cb, cb_d), ("on", on2048, on_d)):
        L[nm] = P.dma("sp", s[:], d, slot="c_" + nm)
    m1 = P.op("dve", lambda e: e.memset(epsb[:], EPS), [])
    hv = h1_d.rearrange("(kc p) t -> p kc t", p=128)
    ov = h2_d.rearrange("(kc p) t -> p kc t", p=128)
    sA, ptA, pdA = pools.get_ps()
    sB, ptB, pdB = pools.get_ps()
    TILES = [(ptA, 0, 0, 512), (ptA, 512, 512, 512), (ptB, 0, 1024, 24)]
    si = 0
    last = None
    for kc in range(16):
        b = si % 2; si += 1
        ld = P.dma("sp", stf[b][:, :], hv[:, kc, :], deps=st_free[b], slot="st%d" % b)
        q = kc % 2
        sq = P.op("act", (lambda e, o=sqb[q][:, :], a=stf[b][:, :]: e.activation(out=o, in_=a, func=AF.Square)), [ld] + sq_free[q])
        st_free[b] = [sq]
        for (pt, po, c0, n) in TILES:
            last = P.op("pe", (lambda e, o=pt[:, po:po + n], r=sqb[q][:, c0:c0 + n], a=(kc == 0), bb=(kc == 15):
                               e.matmul(o, on2048[:, :], r, start=a, stop=bb)), [sq, L["on"]] + ((pdA + pdB) if kc == 0 else []))
        sq_free[q] = [last]
    sops = []
    for (pt, po, c0, n) in TILES:
        sops.append(P.op("act", (lambda e, o=rstd[:, c0:c0 + n], a=pt[:, po:po + n]:
                                 e.activation(out=o, in_=a, func=AF.Sqrt, bias=epsb[:, 0:1], scale=1.0)), [last, m1]))
    pools.ps_free[sA] = list(sops); pools.ps_free[sB] = list(sops)
    r0 = P.op("dve", lambda e: e.reciprocal(out=rstd[:, :], in_=rstd[:, :]), sops)
    hn_ops = []
    for kc in range(16):
        b = si % 2; si += 1
        ld = P.dma("sp", stf[b][:, :], hv[:, kc, :], deps=st_free[b], slot="st%d" % b)
        o_ = P.op("dve", (lambda e, kc=kc, b=b: e.scalar_tensor_tensor(out=hn2[:, kc, :], in0=stf[b][:, :], scalar=gpre[:, kc:kc + 1],
                                                                        in1=rstd[:, :], op0=ALU.mult, op1=ALU.mult)), [r0, L["gpre"], ld])
        st_free[b] = [o_]
        hn_ops.append(o_)
    hn_ready = [hn_ops[-1]]

    f_ops = []
    act_rd = []
    raw_free = [[], []]
    tmp_free = [[], []]
    z0 = P.op("dve", lambda e: e.memset(raw[0][:, 0:2], 0.0), [])
    z1 = P.op("dve", lambda e: e.memset(raw[1][:, 0:2], 0.0), [])
    zdep = [[z0], [z1]]
    for half in range(2):
        if half == 0:
            cts = [(0, 512)]; NA = 512; a0 = 0
            dts = [(0, 512)]
        else:
            cts = [(512, 256), (768, 280)]; NA = 536; a0 = 512
            dts = [(0, 256), (256, 280)]
        act_w = []
        for i in range(44):
            res = []
            wl = ws = None
            for gu in range(2):
                s_, pt, mm, wl, ws = fm_chunk(P, pools, wu_d[i] if gu == 0 else None, 16, 128, hn2, hn_ready, m0=2048 * gu,
                                              wl=wl, ws=ws, cts=cts, wcols=4096)
                res.append((s_, pt, mm))
            for gu in range(2):
                s_, pt, mm = res[gu]
                t = tmp[gu]
                r = raw[gu]
                ci = i * 2 + gu
                pre = []
                if half == 1:
                    pre = [P.op("dve", (lambda e, r=r, ci=ci: e.tensor_copy(out=r[:, 0:2], in_=edge[:, ci, :])), raw_free[gu])]
                evs = []
                off = 2
                for ti, (c0, n) in enumerate(cts):
                    evs.append(P.op("act", (lambda e, o=r[:, off:off + n], a=pt[:, ti * 512:ti * 512 + n]:
                                            e.activation(out=o, in_=a, func=AF.Copy)),
                                    [mm] + ((raw_free[gu] + zdep[gu]) if ti == 0 else [])))
                    off += n
                pools.ps_free[s_] = evs
                w0 = cw[:, ci * 3 + 0:ci * 3 + 1]
                w1 = cw[:, ci * 3 + 1:ci * 3 + 2]
                w2 = cw[:, ci * 3 + 2:ci * 3 + 3]
                bb = cb[:, ci:ci + 1]
                c1 = P.op("dve", (lambda e, t=t, r=r, w2=w2, bb=bb, NA=NA: e.tensor_scalar(out=t[:, 0:NA], in0=r[:, 2:2 + NA], scalar1=w2, scalar2=bb,
                                                                                  op0=ALU.mult, op1=ALU.add)),
                          evs + pre + tmp_free[gu] + [L["cw"], L["cb"]])
                NF = NA if half == 0 else 520
                c2 = P.op("dve", (lambda e, t=t, r=r, w1=w1, NF=NF: e.scalar_tensor_tensor(out=t[:, 0:NF], in0=r[:, 1:1 + NF], scalar=w1, in1=t[:, 0:NF],
                                                                                 op0=ALU.mult, op1=ALU.add)), [c1])
                c3 = P.op("dve", (lambda e, t=t, r=r, w0=w0, NF=NF: e.scalar_tensor_tensor(out=t[:, 0:NF], in0=r[:, 0:NF], scalar=w0, in1=t[:, 0:NF],
                                                                                 op0=ALU.mult, op1=ALU.add)), [c2])
                lastc = c3
                if half == 0:
                    lastc = P.op("dve", (lambda e, r=r, ci=ci: e.tensor_copy(out=edge[:, ci, :], in_=r[:, 512:514])), [c3])
                else:
                    c4 = P.op("dve", (lambda e, t=t, r=r, w1=w1: e.scalar_tensor_tensor(out=t[:, 521:536], in0=r[:, 522:537], scalar=w1, in1=t[:, 521:536],
                                                                                     op0=ALU.mult, op1=ALU.add)), [c3])
                    lastc = P.op("dve", (lambda e, t=t, r=r, w0=w0: e.scalar_tensor_tensor(out=t[:, 522:536], in0=r[:, 522:536], scalar=w0, in1=t[:, 522:536],
                                                                                        op0=ALU.mult, op1=ALU.add)), [c4])
                raw_free[gu] = [lastc]
                res[gu] = lastc
            g_ = P.op("act", lambda e, NA=NA: e.activation(out=tmp[0][:, 0:NA], in_=tmp[0][:, 0:NA], func=AF.Gelu_apprx_tanh), [res[0]])
            a_ = P.op("dve", (lambda e, i=i, NA=NA: e.tensor_tensor(out=act[:, i, 0:NA], in0=tmp[0][:, 0:NA], in1=tmp[1][:, 0:NA], op=ALU.mult)),
                      [g_, res[1]] + act_rd)
            tmp_free[0] = [a_]; tmp_free[1] = [a_]
            act_w.append(a_)
        act_rd = []
        for oc in range(16):
            wsl = []
            wls = []
            for hf in range(2):
                w = wdi % 3; wdi += 1
                wls.append(P.dma("pool", wds[w][:, :], wd_d[oc][:, hf * 22 * 128:(hf + 1) * 22 * 128], deps=wd_free[w], slot="wd%d" % w))
                wsl.append(w)
            s_, pt, pd = pools.get_ps()
            last = None
            for ti, (c0, n) in enumerate(dts):
                for ch in range(44):
                    deps = (wls + act_w[-1:] + pd) if (ti == 0 and ch == 0) else []
                    wt_ = wds[wsl[ch // 22]]
                    cc = (ch % 22) * 128
                    last = P.op("pe", (lambda e, o=pt[:, ti * 512:ti * 512 + n], l=wt_[:, cc:cc + 128],
                                       r=act[:, ch, c0:c0 + n], a=(ch == 0), b=(ch == 43): e.matmul(o, l, r, start=a, stop=b)), deps)
            for w in wsl:
                wd_free[w] = [last]
            evs = []
            for ti, (c0, n) in enumerate(dts):
                evs.append(P.op("act", (lambda e, o=fT[:, oc, a0 + c0:a0 + c0 + n], a=pt[:, ti * 512:ti * 512 + n]:
                                        e.activation(out=o, in_=a, func=AF.Copy)), [last]))
            pools.ps_free[s_] = evs
            f_ops.append(evs[-1])
        act_rd = [last]
    sA, ptA, pdA = pools.get_ps()
    sB, ptB, pdB = pools.get_ps()
    TILES = [(ptA, 0, 0, 512), (ptA, 512, 512, 512), (ptB, 0, 1024, 24)]
    last = None
    for oc in range(16):
        q = oc % 2
        sq = P.op("act", (lambda e, o=sqb[q][:, :], a=fT[:, oc, :]: e.activation(out=o, in_=a, func=AF.Square)), f_ops[-1:] + sq_free[q])
        for (pt, po, c0, n) in TILES:
            last = P.op("pe", (lambda e, o=pt[:, po:po + n], r=sqb[q][:, c0:c0 + n], a=(oc == 0), bb=(oc == 15):
                               e.matmul(o, on2048[:, :], r, start=a, stop=bb)), [sq] + ((pdA + pdB) if oc == 0 else []))
        sq_free[q] = [last]
    sops = []
    for (pt, po, c0, n) in TILES:
        sops.append(P.op("act", (lambda e, o=rstd[:, c0:c0 + n], a=pt[:, po:po + n]:
                                 e.activation(out=o, in_=a, func=AF.Sqrt, bias=epsb[:, 0:1], scale=1.0)), [last] + hn_ops[-1:]))
    r1 = P.op("dve", lambda e: e.reciprocal(out=rstd[:, :], in_=rstd[:, :]), sops)
    fin = []
    for oc in range(16):
        b = si % 2; si += 1
        ld = P.dma("sp", stf[b][:, :], hv[:, oc, :], deps=st_free[b], slot="st%d" % b)
        o1 = P.op("dve", (lambda e, oc=oc: e.scalar_tensor_tensor(out=fT[:, oc, :], in0=fT[:, oc, :], scalar=gpost[:, oc:oc + 1],
                                                                   in1=rstd[:, :], op0=ALU.mult, op1=ALU.mult)), [r1, L["gpost"]])
        o2 = P.op("dve", (lambda e, oc=oc, b=b: e.tensor_tensor(out=stf[b][:, :], in0=stf[b][:, :], in1=fT[:, oc, :], op=ALU.add)), [o1, ld])
        od = P.dma("sp", ov[:, oc, :], stf[b][:, :], deps=[o2], slot="ho%d" % b)
        st_free[b] = [od]
        fin.append(od)
    P.wait("sp", fin)
    if own:
        return P.build()
    return fin


import ml_dtypes
bf = ml_dtypes.bfloat16
TP = 1048
def fmN(w, nk):
    n = w.shape[1]
    if n < 128: w = np.pad(w, ((0, 0), (0, 128 - n)))
    return np.ascontiguousarray(w.reshape(nk, 128, 128).transpose(1, 0, 2).reshape(128, nk * 128))
def tmN(w, nk):
    return np.ascontiguousarray(w.reshape(nk, 128, 512).transpose(1, 0, 2).reshape(128, nk * 512))
def gcol(g):
    return np.ascontiguousarray(g.reshape(-1, 128).T.astype(np.float32))
def prep_k1_weights(inp, l):
    w_in = np.asarray(inp["w_in"][l])
    ch = []
    for j in range(4): ch.append(fmN(w_in[:, 128 * j:128 * j + 128], 16))
    for j in range(4): ch.append(fmN(w_in[:, 512 + 128 * j:512 + 128 * j + 128], 16))
    kr = 1024 + np.arange(64); krs = np.concatenate([kr[32:], kr[:32]])
    ch.append(fmN(w_in[:, np.concatenate([kr, krs])], 16))
    for base in (1088, 2112, 4160):
        for h in range(8): ch.append(fmN(w_in[:, base + 128 * h:base + 128 * h + 128], 16))
    ch.append(fmN(w_in[:, 5184:5192], 16))
    d = {"w1": np.stack(ch)}
    d["w1v"] = np.stack([tmN(w_in[:, 3136 + 512 * g:3136 + 512 * g + 512], 16) for g in range(2)])
    wq_ = np.asarray(inp["w_q_up"][l]); wq = []
    for h in range(8):
        wq.append(fmN(wq_[:, 192 * h:192 * h + 128], 4))
        r = 192 * h + 128 + np.arange(64); rs = np.concatenate([r[32:], r[:32]])
        wq.append(fmN(wq_[:, np.concatenate([r, rs])], 4))
    d["wq"] = np.stack(wq)
    wkv = np.asarray(inp["w_kv_up"][l])
    d["wk"] = np.stack([fmN(wkv[:, 256 * h:256 * h + 128], 4) for h in range(8)])
    wv = []
    for g in range(2):
        cols = np.concatenate([256 * h + 128 + np.arange(128) for h in range(4 * g, 4 * g + 4)])
        wv.append(tmN(wkv[:, cols], 4))
    d["wv"] = np.stack(wv)
    d["g_pre"] = gcol(np.asarray(inp["ln_mix_pre"][l])); d["g_ql"] = gcol(np.asarray(inp["g_q_latent"][l]))
    d["g_kvl"] = gcol(np.asarray(inp["g_kv_latent"][l]))
    d["g_fq"] = gcol(np.asarray(inp["g_fox_q"][l])); d["g_fk"] = gcol(np.asarray(inp["g_fox_k"][l]))
    d["b_f"] = np.asarray(inp["b_forget"][l]).reshape(8, 1).astype(np.float32)
    return d
def consts_k1():
    return {"c_on2048": np.full((128, 128), 1 / 2048, np.float32).astype(bf),
            "c_on512": np.full((128, 128), 1 / 512, np.float32).astype(bf),
            "c_on128": np.full((128, 128), 1 / 128, np.float32).astype(bf)}
def core_pos(c):
    halo = np.arange(8, 16) if c == 0 else 16 + np.arange(1024 * c - 8, 1024 * c)
    return np.concatenate([halo, 16 + 1024 * c + np.arange(1024), np.arange(16)])
def rope_tables(c):
    pos = core_pos(c).astype(np.float32)
    inv = (np.float32(10000.0) ** (-np.arange(32, dtype=np.float32) / np.float32(32))).astype(np.float32)
    ang = (pos[:, None] * inv[None, :]).astype(np.float32)
    cs, sn = np.cos(ang).T.astype(np.float32), np.sin(ang).T.astype(np.float32)
    return {"cos2": np.ascontiguousarray(np.concatenate([cs, cs])), "sin2": np.ascontiguousarray(np.concatenate([-sn, sn]))}
def core_x(inp, c):
    x = np.asarray(inp["x"])[0]; m = np.asarray(inp["meta_tokens"])
    halo = m[8:16] if c == 0 else x[1024 * c - 8:1024 * c]
    return np.ascontiguousarray(np.concatenate([halo, x[1024 * c:1024 * c + 1024], m]).T.astype(np.float32))
def prep_k3a_weights(inp, l):
    wo = np.asarray(inp["w_out"][l])
    return {"wo": np.stack([fmN(wo[:, 128 * oc:128 * oc + 128], 16) for oc in range(16)]),
            "g_post": gcol(np.asarray(inp["ln_mix_post"][l]))}
def prep_k3b_weights(inp, l):
    wu = np.asarray(inp["w_ffn_up"][l]); wd = np.asarray(inp["w_ffn_down"][l])
    wc = np.asarray(inp["w_ffn_conv"][l]); bc = np.asarray(inp["b_ffn_conv"][l])
    d = {}
    d["wu"] = np.stack([np.concatenate([fmN(wu[:, 128 * i:128 * i + 128], 16), fmN(wu[:, 5632 + 128 * i:5632 + 128 * i + 128], 16)], axis=1)
                        for i in range(44)])
    d["wd"] = np.stack([np.ascontiguousarray(wd[:, 128 * oc:128 * oc + 128].reshape(44, 128, 128).transpose(1, 0, 2).reshape(128, 44 * 128))
                        for oc in range(16)])
    cw = wc.reshape(3, 2, 44, 128).transpose(3, 2, 1, 0)
    d["cw"] = np.ascontiguousarray(cw.reshape(128, 44 * 6)).astype(np.float32)
    d["cb"] = np.ascontiguousarray(bc.reshape(2, 44, 128).transpose(2, 1, 0).reshape(128, 88)).astype(np.float32)
    d["g_fpre"] = gcol(np.asarray(inp["ln_ffn_pre"][l])); d["g_fpost"] = gcol(np.asarray(inp["ln_ffn_post"][l]))
    return d


def _run(nc, in_maps):
    return run_bass_kernel_spmd(nc, in_maps, core_ids=list(range(8))).results


def _gather_fm(R1, key, hh, rows=slice(None)):
    parts = [np.asarray(R1[c][key][hh])[rows, 8:1032] for c in range(8)] + [np.asarray(R1[0][key][hh])[rows, 1032:1048]]
    return np.ascontiguousarray(np.concatenate(parts, axis=1))


def _vlay(v):
    out = np.zeros((128, 65, 128), v.dtype)
    out[:, :64, :] = v[:8192].reshape(64, 128, 128).transpose(1, 0, 2)
    out[:16, 64, :] = v[8192:]
    return np.ascontiguousarray(out.reshape(128, 65 * 128))


def _k2_inputs(R1, hh, c2):
    d = dict(c2)
    d["qn"] = _gather_fm(R1, "QM", hh, slice(0, 128)); d["qr"] = _gather_fm(R1, "QM", hh, slice(128, 192))
    d["kn"] = _gather_fm(R1, "KN", hh)
    d["kr"] = np.ascontiguousarray(np.concatenate([np.asarray(R1[c]["KR"])[:, 8:1032] for c in range(8)]
                                                  + [np.asarray(R1[0]["KR"])[:, 1032:1048]], axis=1))
    d["fq"] = _gather_fm(R1, "FQ", hh); d["fk"] = _gather_fm(R1, "FK", hh)
    for key, nm in (("VM", "vm"), ("VF", "vf")):
        v = np.concatenate([np.asarray(R1[c][key])[8:1032, 128 * hh:128 * hh + 128] for c in range(8)]
                           + [np.asarray(R1[0][key])[1032:1048, 128 * hh:128 * hh + 128]], axis=0)
        d[nm] = _vlay(v)
    lfv = np.concatenate([np.asarray(R1[c]["LOGF"])[hh, 8:1032] for c in range(8)]).astype(np.float32)
    lfm = np.asarray(R1[0]["LOGF"])[hh, 1032:1048].astype(np.float32)
    lf = np.zeros((128, 65), np.float32)
    lf[:16, 0] = lfm
    lf[:, 1:] = lfv.reshape(64, 128).T
    d["lf"] = lf
    return d


def _tok_slice(a, c):
    halo = a[..., 8192 + 8:8192 + 16] if c == 0 else a[..., 1024 * c - 8:1024 * c]
    return np.concatenate([halo, a[..., 1024 * c:1024 * c + 1024], a[..., 8192:8208]], axis=-1)


def kernel(**inp):
    inp = {k: np.asarray(v) for k, v in inp.items()}
    nc1 = build_k1(); nc2 = build_k2(); nc3a = build_k3a(); nc3b = build_k3b()
    c1 = consts_k1(); c2 = k2_consts()
    con = {"c_on2048": c1["c_on2048"]}
    ropes = [rope_tables(c) for c in range(8)]
    hT = [core_x(inp, c) for c in range(8)]
    for l in range(4):
        w1 = prep_k1_weights(inp, l)
        R1 = _run(nc1, [dict(w1, **c1, **ropes[c], xT=hT[c]) for c in range(8)])
        del w1
        R2 = _run(nc2, [_k2_inputs(R1, hh, c2) for hh in range(8)])
        om = np.stack([np.asarray(R2[hh]["om"]) for hh in range(8)])
        of = np.stack([np.asarray(R2[hh]["of"]) for hh in range(8)])
        w3a = prep_k3a_weights(inp, l)
        R3a = _run(nc3a, [dict(w3a, **con, xT=hT[c], OM=np.ascontiguousarray(_tok_slice(om, c)),
                               OF=np.ascontiguousarray(_tok_slice(of, c)), GATE=np.asarray(R1[c]["GATE"])) for c in range(8)])
        del w3a, R1, R2, om, of
        w3b = prep_k3b_weights(inp, l)
        R3b = _run(nc3b, [dict(w3b, **con, h1T=np.asarray(R3a[c]["h1T"])) for c in range(8)])
        del w3b
        hT = [np.asarray(R3b[c]["h2T"]) for c in range(8)]
    out = np.concatenate([hT[c][:, 8:1032].T for c in range(8)], axis=0)[None]
    return np.ascontiguousarray(out.astype(np.float32))
```

```python
import os
from concourse.bass_utils import run_bass_kernel_spmd


from contextlib import ExitStack
import numpy as np
import concourse.bass as bass
import concourse.mybir as mybir

F32 = mybir.dt.float32
BF16 = mybir.dt.bfloat16
AF = mybir.ActivationFunctionType
ALU = mybir.AluOpType
AX = mybir.AxisListType

ENGS = ("pe", "act", "dve", "pool", "sp")
EPOCH = 12000


class Op:
    __slots__ = ("eng", "fn", "deps", "sig", "is_dma", "slot", "used", "idx")

    def __init__(self, eng, fn, deps, is_dma, slot):
        self.eng = eng
        self.fn = fn
        self.deps = [d for d in deps if d is not None]
        self.sig = None
        self.is_dma = is_dma
        self.slot = slot
        self.used = False


class Prog:
    def __init__(self):
        self.nc = bass.Bass("TRN2", target_bir_lowering=False)
        self.ops = {e: [] for e in ENGS}
        self.ctx = ExitStack()
        self.nsem = 0
        self._names = set()

    def dram(self, name, shape, dt, kind):
        return self.nc.dram_tensor(name, list(shape), dt, kind=kind).ap()

    def sb(self, name, shape, dt):
        return self.ctx.enter_context(self.nc.sbuf_tensor(name, list(shape), dt))

    def ps(self, name, shape, dt=F32):
        return self.ctx.enter_context(self.nc.psum_tensor(name, list(shape), dt))

    def _sem(self, name):
        self.nsem += 1
        return self.ctx.enter_context(self.nc.semaphore(name))

    def op(self, eng, fn, deps=()):
        o = Op(eng, fn, deps, False, None)
        self.ops[eng].append(o)
        return o

    def dma(self, eng, out, in_, deps=(), slot=None, **kw):
        assert slot is not None
        try:
            shp = list(out.shape)
            shp_i = list(in_.shape)
        except Exception:
            shp = shp_i = None
        if shp is not None and len(shp) == 2 and shp == shp_i and shp[1] > 1024:
            last = None
            for c0 in range(0, shp[1], 1024):
                c1 = min(shp[1], c0 + 1024)
                last = self.dma(eng, out[:, c0:c1], in_[:, c0:c1], deps=deps, slot=slot, **kw)
            return last
        o = Op(eng, (lambda e, out=out, in_=in_, kw=kw: e.dma_start(out=out, in_=in_, **kw)), deps, True, slot)
        self.ops[eng].append(o)
        return o

    def wait(self, eng, deps):
        o = Op(eng, None, deps, False, None)
        self.ops[eng].append(o)
        return o

    def build(self):
        nc = self.nc
        for e in ENGS:
            for o in self.ops[e]:
                for d in o.deps:
                    d.used = True
        slot_sems = {}
        for e in ENGS:
            cnt = 0
            sem = None
            for o in self.ops[e]:
                if o.fn is None:
                    continue
                if o.is_dma:
                    if o.slot not in slot_sems:
                        slot_sems[o.slot] = [self._sem("d_%s" % o.slot), 0]
                    ss = slot_sems[o.slot]
                    if ss[1] >= 16 * 1800:
                        ss[0] = self._sem("d_%s_%d" % (o.slot, self.nsem))
                        ss[1] = 0
                    ss[1] += 16
                    o.sig = (ss[0], ss[1], 16)
                elif o.used:
                    if sem is None or cnt >= EPOCH:
                        sem = self._sem("e_%s_%d" % (e, self.nsem))
                        cnt = 0
                    cnt += 1
                    o.sig = (sem, cnt, 1)
        with nc.Block() as block:
            def emit(e, engobj):
                waited = {}
                for o in self.ops[e]:
                    for d in o.deps:
                        sem, val, _ = d.sig
                        k = id(sem)
                        if waited.get(k, 0) < val:
                            engobj.wait_ge(sem, val)
                            waited[k] = val
                    if o.fn is None:
                        continue
                    ins = o.fn(engobj)
                    if o.sig is not None:
                        ins.then_inc(o.sig[0], o.sig[2])

            @block.tensor
            def _(t):
                emit("pe", t)

            @block.scalar
            def _(s):
                emit("act", s)

            @block.vector
            def _(v):
                emit("dve", v)

            @block.gpsimd
            def _(g):
                emit("pool", g)

            @block.sync
            def _(s):
                emit("sp", s)
        self.ctx.close()
        return nc


TP = 1048
CT = [(0, 512), (512, 512), (1024, 24)]
EPS = 1e-6
SC_M = 192.0 ** -0.5
SC_F = 128.0 ** -0.5


class Pools:
    def __init__(self, P, nps=2, nw=3, wsize=16 * 128, banks=3):
        self.P = P
        self.ps = [P.ps("pp%d" % i, [128, banks * 512]) for i in range(nps)]
        self.ps_free = [[] for _ in range(nps)]
        self.ps_i = 0
        self.w = [P.sb("wsl%d" % i, [128, wsize], BF16) for i in range(nw)]
        self.w_free = [[] for _ in range(nw)]
        self.w_i = 0

    def get_ps(self):
        s = self.ps_i % len(self.ps)
        self.ps_i += 1
        d = self.ps_free[s]
        self.ps_free[s] = []
        return s, self.ps[s], d

    def get_w(self):
        s = self.w_i % len(self.w)
        self.w_i += 1
        d = self.w_free[s]
        self.w_free[s] = []
        return s, self.w[s], d


def fm_chunk(P, pools, wsrc, nk, M, src, src_ready, m0=0, wl=None, ws=None, cts=CT, wcols=None):
    if wl is None:
        ws, wt, wd = pools.get_w()
        wc_ = wcols if wcols is not None else nk * 128
        wl = P.dma("pool", wt[:, 0:wc_], wsrc, deps=wd, slot="w%d" % ws)
    wt = pools.w[ws]
    s, pt, pd = pools.get_ps()
    last = None
    first = True
    for ti, (c0, n) in enumerate(cts):
        for kc in range(nk):
            deps = ([wl] + list(src_ready) + pd) if first else []
            first = False
            last = P.op("pe", (lambda e, o=pt[0:M, ti * 512:ti * 512 + n], l=wt[:, kc * 128 + m0:kc * 128 + m0 + M],
                               r=src[:, kc, c0:c0 + n], a=(kc == 0), b=(kc == nk - 1):
                               e.matmul(o, l, r, start=a, stop=b)), deps)
    pools.w_free[ws] = [last]
    return s, pt, last, wl, ws


def ps_view(pt, M, cts=CT):
    return [pt[0:M, ti * 512:ti * 512 + n] for ti, (c0, n) in enumerate(cts)]


def rms_stats(P, pools, srcs, src_ready, onesw, ones_ready, sq_bufs, sq_free, rstd_out, eps_ap, scale=1.0, cts=CT, wdeps=()):
    s, pt, pd = pools.get_ps()
    nsrc = len(srcs)
    last = None
    for i, sap in enumerate(srcs):
        b = i % len(sq_bufs)
        sq = P.op("act", (lambda e, o=sq_bufs[b][:, 0:TP], a=sap: e.activation(out=o, in_=a, func=AF.Square)),
                  list(src_ready) + sq_free[b])
        for ti, (c0, n) in enumerate(cts):
            deps = [sq, ones_ready] + (pd if i == 0 else [])
            last = P.op("pe", (lambda e, o=pt[:, ti * 512:ti * 512 + n], r=sq_bufs[b][:, c0:c0 + n], a=(i == 0), bb=(i == nsrc - 1):
                               e.matmul(o, onesw, r, start=a, stop=bb)), deps)
        sq_free[b] = [last]
    k = 1.0 / (scale * scale)
    ops = []
    for ti, (c0, n) in enumerate(cts):
        o1 = P.op("act", (lambda e, o=rstd_out[:, c0:c0 + n], a=pt[:, ti * 512:ti * 512 + n]:
                          e.activation(out=o, in_=a, func=AF.Sqrt, bias=eps_ap, scale=k)), [last] + list(wdeps))
        ops.append(o1)
    pools.ps_free[s] = list(ops)
    r = P.op("dve", lambda e: e.reciprocal(out=rstd_out[:, 0:TP], in_=rstd_out[:, 0:TP]), ops)
    return r


def build_k1(P=None, io=None):
    own = P is None
    if own:
        P = Prog()
    io = io or {}

    def din(name, shape, dt):
        return io[name] if name in io else P.dram(name, shape, dt, "ExternalInput")

    def dout(name, shape, dt):
        return io[name] if name in io else P.dram(name, shape, dt, "ExternalOutput")

    xT_d = din("xT", [2048, TP], F32)
    w1_d = din("w1", [34, 128, 16 * 128], F32)
    w1v_d = din("w1v", [2, 128, 16 * 512], F32)
    wq_d = din("wq", [16, 128, 4 * 128], F32)
    wk_d = din("wk", [8, 128, 4 * 128], F32)
    wv_d = din("wv", [2, 128, 4 * 512], F32)
    gpre_d = din("g_pre", [128, 16], F32)
    gq_d = din("g_ql", [128, 4], F32)
    gkv_d = din("g_kvl", [128, 4], F32)
    gfq_d = din("g_fq", [128, 1], F32)
    gfk_d = din("g_fk", [128, 1], F32)
    bf_d = din("b_f", [8, 1], F32)
    cos_d = din("cos2", [64, TP], F32)
    sin_d = din("sin2", [64, TP], F32)
    on2048_d = din("c_on2048", [128, 128], BF16)
    on512_d = din("c_on512", [128, 128], BF16)
    on128_d = din("c_on128", [128, 128], BF16)

    QM_d = dout("QM", [8, 192, TP], BF16)
    KN_d = dout("KN", [8, 128, TP], BF16)
    KR_d = dout("KR", [64, TP], BF16)
    VM_d = dout("VM", [TP, 1024], BF16)
    FQ_d = dout("FQ", [8, 128, TP], BF16)
    FK_d = dout("FK", [8, 128, TP], BF16)
    VF_d = dout("VF", [TP, 1024], BF16)
    LOGF_d = dout("LOGF", [8, TP], F32)
    GATE_d = dout("GATE", [8, 128, TP], F32)

    xst = [P.sb("xst%d" % i, [128, TP], F32) for i in range(3)]
    xst_free = [[], [], []]
    hn = P.sb("hn", [128, 16, TP], BF16)
    cq = P.sb("cq", [128, 4, TP], F32)
    ckv = P.sb("ckv", [128, 4, TP], F32)
    cqn = P.sb("cqn", [128, 4, TP], BF16)
    ckvn = P.sb("ckvn", [128, 4, TP], BF16)
    rstd = P.sb("rstd", [128, TP], F32)
    rstd2 = P.sb("rstd2", [128, TP], F32)
    sqb = [P.sb("sqb%d" % i, [128, TP], BF16) for i in range(2)]
    sq_free = [[], []]
    gpre = P.sb("gpre", [128, 16], F32); gq = P.sb("gq", [128, 4], F32); gkv = P.sb("gkv", [128, 4], F32)
    gfq = P.sb("gfq", [128, 1], F32); gfk = P.sb("gfk", [128, 1], F32); bfb = P.sb("bfb", [8, 1], F32)
    cos2 = P.sb("cos2s", [64, TP], F32); sin2 = P.sb("sin2s", [64, TP], F32)
    on2048 = P.sb("on2048", [128, 128], BF16); on512 = P.sb("on512", [128, 128], BF16); on128 = P.sb("on128", [128, 128], BF16)
    epsb = P.sb("epsb", [128, 1], F32); epsf = P.sb("epsf", [128, 1], F32)
    NE = 3
    ev_f = [P.sb("evf%d" % i, [128, TP], F32) for i in range(NE)]
    ev_b = [P.sb("evb%d" % i, [128, TP], BF16) for i in range(NE)]
    evf_free = [[] for _ in range(NE)]
    evb_free = [[] for _ in range(NE)]
    cnt = {"f": 0, "b": 0}
    tmv = [P.sb("tmv%d" % i, [128, 512], BF16) for i in range(2)]
    tmv_free = [[], []]
    wvs = P.sb("wvs", [128, 16 * 512], BF16)
    t1 = P.sb("t1", [64, TP], F32); t2 = P.sb("t2", [64, TP], F32)

    pools = Pools(P)

    def get_evf():
        s = cnt["f"] % NE; cnt["f"] += 1
        d = evf_free[s]; evf_free[s] = []
        return s, ev_f[s], d

    def get_evb():
        s = cnt["b"] % NE; cnt["b"] += 1
        d = evb_free[s]; evb_free[s] = []
        return s, ev_b[s], d

    L = {}
    for nm, s, d in (("gpre", gpre, gpre_d), ("gq", gq, gq_d), ("gkv", gkv, gkv_d), ("gfq", gfq, gfq_d), ("gfk", gfk, gfk_d),
                     ("bf", bfb, bf_d), ("cos", cos2, cos_d), ("sin", sin2, sin_d), ("on2048", on2048, on2048_d),
                     ("on512", on512, on512_d), ("on128", on128, on128_d)):
        L[nm] = P.dma("sp", s[:], d, slot="c_" + nm)
    m1 = P.op("dve", lambda e: e.memset(epsb[:], EPS), [])
    m2 = P.op("dve", lambda e: e.memset(epsf[:], EPS / (SC_F * SC_F)), [])

    xv = xT_d.rearrange("(kc p) t -> p kc t", p=128)
    s0, pt0, pd0 = pools.get_ps()
    last = None
    xi = 0
    for kc in range(16):
        b = xi % 3; xi += 1
        ld = P.dma("sp", xst[b][:, :], xv[:, kc, :], deps=xst_free[b], slot="x%d" % b)
        sb_ = kc % 2
        sq = P.op("act", (lambda e, o=sqb[sb_][:, 0:TP], a=xst[b][:, :]: e.activation(out=o, in_=a, func=AF.Square)),
                  [ld] + sq_free[sb_])
        xst_free[b] = [sq]
        for ti, (c0, n) in enumerate(CT):
            last = P.op("pe", (lambda e, o=pt0[:, ti * 512:ti * 512 + n], r=sqb[sb_][:, c0:c0 + n], a=(kc == 0), bb=(kc == 15):
                               e.matmul(o, on2048[:, :], r, start=a, stop=bb)), [sq, L["on2048"]] + (pd0 if kc == 0 else []))
        sq_free[sb_] = [last]
    sops = []
    for ti, (c0, n) in enumerate(CT):
        sops.append(P.op("act", (lambda e, o=rstd[:, c0:c0 + n], a=pt0[:, ti * 512:ti * 512 + n]:
                                 e.activation(out=o, in_=a, func=AF.Sqrt, bias=epsb[:, 0:1], scale=1.0)), [last, m1]))
    pools.ps_free[s0] = list(sops)
    r0 = P.op("dve", lambda e: e.reciprocal(out=rstd[:, 0:TP], in_=rstd[:, 0:TP]), sops)
    hn_ops = []
    for kc in range(16):
        b = xi % 3; xi += 1
        ld = P.dma("sp", xst[b][:, :], xv[:, kc, :], deps=xst_free[b], slot="x%d" % b)
        o_ = P.op("dve", (lambda e, kc=kc, b=b: e.scalar_tensor_tensor(
            out=hn[:, kc, :], in0=xst[b][:, :], scalar=gpre[:, kc:kc + 1], in1=rstd[:, :], op0=ALU.mult, op1=ALU.mult)),
            [r0, L["gpre"], ld])
        xst_free[b] = [o_]
        hn_ops.append(o_)
    hn_ready = [hn_ops[-1]]
    STOP = int(os.environ.get('K1_STOP', '99'))
    if STOP <= 1:
        P.wait('sp', hn_ops[-1:]); return P.build()

    def out_dma(dst, src, dep, buf_kind, slot_i):
        o = P.dma("sp", dst, src, deps=[dep], slot="o%s%d" % (buf_kind, slot_i))
        if buf_kind == "b":
            evb_free[slot_i] = [o]
        else:
            evf_free[slot_i] = [o]
        return o

    final = []

    for which, dst in ((0, cq), (1, ckv)):
        for j in range(4):
            s, pt, mm, _, _ = fm_chunk(P, pools, w1_d[which * 4 + j], 16, 128, hn, hn_ready)
            evs = []
            for ti, (c0, n) in enumerate(CT):
                evs.append(P.op("act", (lambda e, o=dst[:, j, c0:c0 + n], a=pt[:, ti * 512:ti * 512 + n]:
                                        e.activation(out=o, in_=a, func=AF.Copy)), [mm]))
            pools.ps_free[s] = evs
            if which == 0 and j == 3:
                cq_ready = evs
            if which == 1 and j == 3:
                ckv_ready = evs
    if STOP <= 2:
        P.wait('sp', cq_ready + ckv_ready); return P.build()
    s1, pt1, mm1, wl, ws = fm_chunk(P, pools, w1_d[8], 16, 64, hn, hn_ready, m0=0)
    s2, pt2, mm2, _, _ = fm_chunk(P, pools, None, 16, 64, hn, hn_ready, m0=64, wl=wl, ws=ws)

    def rope_combine(pt1, s1, mm1, pt2, s2, mm2, dst_dram, scale):
        a = P.op("dve", lambda e: e.tensor_tensor(out=t1[:, :], in0=pt1_v(pt1), in1=cos2[:, :], op=ALU.mult), [mm1, L["cos"]])
        return a

    def rope(pt1, s1, mm1, pt2, s2, mm2, dst_dram, scale, extra_free):
        o_a = []
        o_b = []
        for ti, (c0, n) in enumerate(CT):
            o_a.append(P.op("dve", (lambda e, c0=c0, n=n, ti=ti: e.tensor_tensor(
                out=t1[:, c0:c0 + n], in0=pt1[0:64, ti * 512:ti * 512 + n], in1=cos2[:, c0:c0 + n], op=ALU.mult)),
                [mm1, L["cos"]] + extra_free))
            o_b.append(P.op("dve", (lambda e, c0=c0, n=n, ti=ti: e.tensor_tensor(
                out=t2[:, c0:c0 + n], in0=pt2[0:64, ti * 512:ti * 512 + n], in1=sin2[:, c0:c0 + n], op=ALU.mult)),
                [mm2, L["sin"]]))
        pools.ps_free[s1] = o_a
        pools.ps_free[s2] = o_b
        bs, bt, bd = get_evb()
        oc = P.op("dve", lambda e: e.scalar_tensor_tensor(out=bt[0:64, :], in0=t1[:, :], scalar=scale, in1=t2[:, :],
                                                          op0=ALU.mult, op1=ALU.add), o_a + o_b + bd)
        return bs, bt, oc

    def rope2(pt1, s1, mm1, pt2, s2, mm2, scale, extra_free):
        o_a = []
        o_b = []
        for ti, (c0, n) in enumerate(CT):
            o_a.append(P.op("dve", (lambda e, c0=c0, n=n, ti=ti: e.tensor_tensor(
                out=t1[:, c0:c0 + n], in0=pt1[0:64, ti * 512:ti * 512 + n], in1=cos2[:, c0:c0 + n], op=ALU.mult)),
                [mm1, L["cos"]] + extra_free))
        pools.ps_free[s1] = o_a
        for ti, (c0, n) in enumerate(CT):
            o_b.append(P.op("dve", (lambda e, c0=c0, n=n, ti=ti: e.tensor_tensor(
                out=t2[:, c0:c0 + n], in0=pt2[0:64, ti * 512:ti * 512 + n], in1=sin2[:, c0:c0 + n], op=ALU.mult)),
                [mm2, L["sin"]] + o_a))
        pools.ps_free[s2] = o_b
        o_c = P.op("dve", lambda e: e.tensor_tensor(out=t1[:, :], in0=t1[:, :], in1=t2[:, :], op=ALU.add), o_b)
        bs, bt, bd = get_evb()
        o_d = P.op("act", lambda e: e.activation(out=bt[0:64, :], in_=t1[:, :], func=AF.Identity, scale=scale), [o_c] + bd)
        return bs, bt, o_d

    bs, bt, od = rope2(pt1, s1, mm1, pt2, s2, mm2, 1.0, [])
    rope_prev = [od]
    final.append(out_dma(KR_d[:, :], bt[0:64, :], od, "b", bs))

    if STOP <= 3:
        P.wait('sp', final); return P.build()
    rstd2_rd = []
    for which, dst_d, gain, gl, sc, eps_ap in ((0, FQ_d, gfq, "gfq", SC_F, epsf), (1, FK_d, gfk, "gfk", 1.0, epsb)):
        for h in range(8):
            s, pt, mm, _, _ = fm_chunk(P, pools, w1_d[9 + which * 8 + h], 16, 128, hn, hn_ready)
            fs, ft, fd = get_evf()
            evs = []
            for ti, (c0, n) in enumerate(CT):
                evs.append(P.op("act", (lambda e, o=ft[:, c0:c0 + n], a=pt[:, ti * 512:ti * 512 + n]:
                                        e.activation(out=o, in_=a, func=AF.Copy)), [mm] + (fd if ti == 0 else [])))
            pools.ps_free[s] = evs
            rr = rms_stats(P, pools, [ft[:, :]], evs, on128[:, :], L["on128"], sqb, sq_free, rstd2, eps_ap[:, 0:1], scale=sc, wdeps=[m1, m2] + rstd2_rd)
            bs, bt, bd = get_evb()
            on_ = P.op("dve", (lambda e, bt=bt, ft=ft, gain=gain: e.scalar_tensor_tensor(
                out=bt[:, :], in0=ft[:, :], scalar=gain[:, 0:1], in1=rstd2[:, :], op0=ALU.mult, op1=ALU.mult)),
                [rr, L[gl], m2] + bd)
            evf_free[fs] = [on_]
            rstd2_rd = [on_]
            final.append(out_dma(dst_d[h], bt[:, :], on_, "b", bs))
    if STOP <= 4:
        P.wait('sp', final); return P.build()
    for h in range(8):
        s, pt, mm, _, _ = fm_chunk(P, pools, w1_d[25 + h], 16, 128, hn, hn_ready)
        fs, ft, fd = get_evf()
        evs = []
        for ti, (c0, n) in enumerate(CT):
            evs.append(P.op("act", (lambda e, o=ft[:, c0:c0 + n], a=pt[:, ti * 512:ti * 512 + n]:
                                    e.activation(out=o, in_=a, func=AF.Sigmoid)), [mm] + (fd if ti == 0 else [])))
        pools.ps_free[s] = evs
        final.append(out_dma(GATE_d[h], ft[:, :], evs[-1], "f", fs))
        evf_free[fs] = [final[-1]]
    s, pt, mm, _, _ = fm_chunk(P, pools, w1_d[33], 16, 8, hn, hn_ready)
    fs, ft, fd = get_evf()
    evs = []
    for ti, (c0, n) in enumerate(CT):
        evs.append(P.op("act", (lambda e, o=ft[0:8, c0:c0 + n], a=pt[0:8, ti * 512:ti * 512 + n]:
                                e.activation(out=o, in_=a, func=AF.Sigmoid, bias=bfb[:, 0:1], scale=1.0)),
                        [mm, L["bf"]] + (fd if ti == 0 else [])))
    pools.ps_free[s] = evs
    lg = P.op("act", lambda e: e.activation(out=ft[0:8, :], in_=ft[0:8, :], func=AF.Ln), evs)
    final.append(out_dma(LOGF_d[:, :], ft[0:8, :], lg, "f", fs))

    if STOP <= 5:
        P.wait('sp', final); return P.build()
    TT = [(128 * i, 128) for i in range(8)] + [(1024, 24)]

    def tm_group(wsrc, nk, src, src_ready, dst_d, col0, wfree):
        wl = P.dma("pool", wvs[:, 0:nk * 512], wsrc, deps=wfree, slot="wv")
        last = None
        for ti, (c0, n) in enumerate(TT):
            s, pt, pd = pools.get_ps()
            for kc in range(nk):
                deps = ([wl] + list(src_ready) + pd) if kc == 0 else []
                last = P.op("pe", (lambda e, o=pt[0:n, 0:512], l=src[:, kc, c0:c0 + n], r=wvs[:, kc * 512:(kc + 1) * 512],
                                   a=(kc == 0), b=(kc == nk - 1): e.matmul(o, l, r, start=a, stop=b)), deps)
            b = ti % 2
            ev = P.op("act", (lambda e, o=tmv[b][0:n, :], a=pt[0:n, 0:512]: e.activation(out=o, in_=a, func=AF.Copy)),
                      [last] + tmv_free[b])
            pools.ps_free[s] = [ev]
            od = P.dma("sp", dst_d[c0:c0 + n, col0:col0 + 512], tmv[b][0:n, :], deps=[ev], slot="otm%d" % b)
            tmv_free[b] = [od]
            final.append(od)
        return [last]

    wfree = []
    for g in range(2):
        wfree = tm_group(w1v_d[g], 16, hn, hn_ready, VF_d, 512 * g, wfree)

    if STOP <= 6:
        P.wait('sp', final); return P.build()
    rq = rms_stats(P, pools, [cq[:, j, :] for j in range(4)], cq_ready, on512[:, :], L["on512"], sqb, sq_free, rstd, epsb[:, 0:1], wdeps=hn_ops[-1:])
    qn_ops = [P.op("dve", (lambda e, j=j: e.scalar_tensor_tensor(out=cqn[:, j, :], in0=cq[:, j, :], scalar=gq[:, j:j + 1],
                                                                  in1=rstd[:, :], op0=ALU.mult, op1=ALU.mult)),
                   [rq, L["gq"]]) for j in range(4)]
    rk = rms_stats(P, pools, [ckv[:, j, :] for j in range(4)], ckv_ready, on512[:, :], L["on512"], sqb, sq_free, rstd2, epsb[:, 0:1], wdeps=rstd2_rd)
    kn_ops = [P.op("dve", (lambda e, j=j: e.scalar_tensor_tensor(out=ckvn[:, j, :], in0=ckv[:, j, :], scalar=gkv[:, j:j + 1],
                                                                  in1=rstd2[:, :], op0=ALU.mult, op1=ALU.mult)),
                   [rk, L["gkv"]]) for j in range(4)]
    cqn_ready = [qn_ops[-1]]
    ckvn_ready = [kn_ops[-1]]
    for h in range(8):
        s, pt, mm, _, _ = fm_chunk(P, pools, wq_d[2 * h], 4, 128, cqn, cqn_ready)
        bs, bt, bd = get_evb()
        evs = []
        for ti, (c0, n) in enumerate(CT):
            evs.append(P.op("act", (lambda e, o=bt[:, c0:c0 + n], a=pt[:, ti * 512:ti * 512 + n]:
                                    e.activation(out=o, in_=a, func=AF.Identity, scale=SC_M)), [mm] + (bd if ti == 0 else [])))
        pools.ps_free[s] = evs
        final.append(out_dma(QM_d[h, 0:128, :], bt[:, :], evs[-1], "b", bs))
        s1, pt1, mm1, wl, ws = fm_chunk(P, pools, wq_d[2 * h + 1], 4, 64, cqn, cqn_ready, m0=0)
        s2, pt2, mm2, _, _ = fm_chunk(P, pools, None, 4, 64, cqn, cqn_ready, m0=64, wl=wl, ws=ws)
        bs, bt, od = rope2(pt1, s1, mm1, pt2, s2, mm2, SC_M, rope_prev)
        rope_prev = [od]
        final.append(out_dma(QM_d[h, 128:192, :], bt[0:64, :], od, "b", bs))
    for h in range(8):
        s, pt, mm, _, _ = fm_chunk(P, pools, wk_d[h], 4, 128, ckvn, ckvn_ready)
        bs, bt, bd = get_evb()
        evs = []
        for ti, (c0, n) in enumerate(CT):
            evs.append(P.op("act", (lambda e, o=bt[:, c0:c0 + n], a=pt[:, ti * 512:ti * 512 + n]:
                                    e.activation(out=o, in_=a, func=AF.Copy)), [mm] + (bd if ti == 0 else [])))
        pools.ps_free[s] = evs
        final.append(out_dma(KN_d[h], bt[:, :], evs[-1], "b", bs))
    for g in range(2):
        wfree = tm_group(wv_d[g], 4, ckvn, ckvn_ready, VM_d, 512 * g, wfree)

    P.wait("sp", final)
    if own:
        return P.build()
    return final


NQB = 16
NB = 4 * NQB
TG = 512 * NQB + 16
FR = 512 * NQB
NBB = NB + 1
MASKV = -30000.0


def k2_consts():
    import ml_dtypes
    bf = ml_dtypes.bfloat16
    p = np.arange(128)[:, None]
    col = np.arange(512)[None, :]
    mf = np.zeros((128, 4, 512), np.float32)
    mm = np.zeros((128, 4, 512), np.float32)
    for i in range(4):
        k = 128 * i + p
        mf[:, i, :] = np.where(k <= col, 0.0, MASKV)
        mm[:, i, :] = np.where((k // 64) <= (col // 64), 0.0, MASKV)
    mmeta = np.where(np.arange(16)[:, None] <= np.arange(16)[None, :], 0.0, MASKV).astype(np.float32)
    tri = (np.arange(128)[:, None] <= np.arange(128)[None, :]).astype(np.float32)
    st = (np.arange(NBB)[:, None] < np.arange(NBB)[None, :]).astype(np.float32)
    return {
        "c_maskf": mf.reshape(128, 2048).astype(bf),
        "c_maskm": mm.reshape(128, 2048).astype(bf),
        "c_maskmeta": mmeta.astype(bf),
        "c_tri": tri,
        "c_st": st,
        "c_identf": np.eye(128, dtype=np.float32),
        "c_identb": np.eye(128, dtype=np.float32).astype(bf),
        "c_onesb": np.ones((128, 128), np.float32).astype(bf),
        "c_onesf": np.ones((128, 128), np.float32),
    }


def build_k2(P=None, io=None):
    own = P is None
    if own:
        P = Prog()
    nc = P.nc
    D = {}

    def din(name, shape, dt):
        if io is not None and name in io:
            return io[name]
        return P.dram(name, shape, dt, "ExternalInput")

    qn_d = din("qn", [128, TG], BF16); qr_d = din("qr", [64, TG], BF16)
    kn_d = din("kn", [128, TG], BF16); kr_d = din("kr", [64, TG], BF16)
    vm_d = din("vm", [128, NBB * 128], BF16)
    fq_d = din("fq", [128, TG], BF16); fk_d = din("fk", [128, TG], BF16)
    vf_d = din("vf", [128, NBB * 128], BF16)
    lf_d = din("lf", [128, NBB], F32)
    cmf_d = din("c_maskf", [128, 2048], BF16); cmm_d = din("c_maskm", [128, 2048], BF16)
    cmeta_d = din("c_maskmeta", [16, 16], BF16)
    tri_d = din("c_tri", [128, 128], F32); st_d = din("c_st", [NBB, NBB], F32)
    idf_d = din("c_identf", [128, 128], F32); idb_d = din("c_identb", [128, 128], BF16)
    onb_d = din("c_onesb", [128, 128], BF16); onf_d = din("c_onesf", [128, 128], F32)
    if io is not None and "om" in io:
        om_d, of_d = io["om"], io["of"]
    else:
        om_d = P.dram("om", [128, TG], F32, "ExternalOutput")
        of_d = P.dram("of", [128, TG], F32, "ExternalOutput")

    qn = P.sb("qn_s", [128, TG], BF16); qr = P.sb("qr_s", [64, TG], BF16)
    kn = P.sb("kn_s", [128, TG], BF16); kr = P.sb("kr_s", [64, TG], BF16)
    vm = P.sb("vm_s", [128, NBB * 128], BF16)
    fq = P.sb("fq_s", [128, TG], BF16); fk = P.sb("fk_s", [128, TG], BF16)
    vf = P.sb("vf_s", [128, NBB * 128], BF16)
    lf = P.sb("lf_s", [128, NBB], F32)
    cmf = P.sb("cmf_s", [128, 2048], BF16); cmm = P.sb("cmm_s", [128, 2048], BF16)
    cmeta = P.sb("cmeta_s", [16, 16], BF16)
    tri = P.sb("tri_s", [128, 128], F32); st = P.sb("st_s", [NBB, NBB], F32)
    idf = P.sb("idf_s", [128, 128], F32); idb = P.sb("idb_s", [128, 128], BF16)
    onb = P.sb("onb_s", [128, 128], BF16); onf = P.sb("onf_s", [128, 128], F32)
    xs = P.sb("xs_s", [NBB, 128], F32)
    c_sb = P.sb("c_sb", [128, NBB], F32)
    negc = P.sb("negc", [128, NBB], F32)
    ct = P.sb("ct", [NBB, 128], F32)
    chi = P.sb("chi", [NBB, 128], BF16); cmi = P.sb("cmi", [NBB, 128], BF16); clo = P.sb("clo", [NBB, 128], BF16)
    r1 = P.sb("r1", [NBB, 128], F32); r2 = P.sb("r2", [NBB, 128], F32)
    crow = P.sb("crow", [3, NBB * 128], BF16)
    NS = 3
    p_sb = [P.sb("p_sb%d" % i, [128, 512], BF16) for i in range(NS)]
    o_sb = [P.sb("o_sb%d" % i, [128, 512], F32) for i in range(2)]
    rden = [P.sb("rden%d" % i, [128, 512], F32) for i in range(2)]
    s_ps = [P.ps("s_ps%d" % i, [128, 512]) for i in range(NS)]
    o_ps = [P.ps("o_ps%d" % i, [128, 512]) for i in range(2)]
    d_ps = [P.ps("d_ps%d" % i, [128, 512]) for i in range(2)]

    L = {}
    L["lf"] = P.dma("sp", lf[:], lf_d, slot="lf")
    for nm, s, d in (("tri", tri, tri_d), ("st", st, st_d), ("idf", idf, idf_d), ("onf", onf, onf_d),
                     ("idb", idb, idb_d), ("onb", onb, onb_d), ("cmeta", cmeta, cmeta_d),
                     ("cmf", cmf, cmf_d), ("cmm", cmm, cmm_d)):
        L[nm] = P.dma("act", s[:], d, slot="c_" + nm)
    for nm, s, d in (("qn", qn, qn_d), ("kn", kn, kn_d), ("qr", qr, qr_d), ("kr", kr, kr_d), ("vm", vm, vm_d),
                     ("fq", fq, fq_d), ("fk", fk, fk_d), ("vf", vf, vf_d)):
        ncol = s.shape[1]
        for c0 in range(0, ncol, 1024):
            c1 = min(ncol, c0 + 1024)
            L[nm] = P.dma("sp", s[:, c0:c1], d[:, c0:c1], slot="in_" + nm)

    TYPES = 'mf'
    prolog_free = {0: None, 1: None, 2: None}
    crow_ready = None
    o6 = None
    if 'f' in TYPES:
        x_ps = s_ps[0]
        o1 = P.op("pe", lambda e: e.matmul(x_ps[0:NBB, 0:128], lf[:, :], onf[:, :], start=True, stop=True),
                  [L["lf"], L["onf"]])
        o2 = P.op("dve", lambda e: e.tensor_copy(out=xs[:, :], in_=x_ps[0:NBB, 0:128]), [o1])
        c_ps = s_ps[1]
        o3 = P.op("pe", lambda e: e.matmul(c_ps[:, 0:NBB], tri[:, :], lf[:, :], start=True, stop=False), [L["tri"], L["lf"]])
        o4 = P.op("pe", lambda e: e.matmul(c_ps[:, 0:NBB], xs[:, :], st[:, :], start=False, stop=True), [o2, L["st"], o3])
        o5 = P.op("dve", lambda e: e.tensor_copy(out=c_sb[:, :], in_=c_ps[:, 0:NBB]), [o4])
        o6 = P.op("dve", lambda e: e.tensor_scalar(out=negc[:, :], in0=c_sb[:, :], scalar1=-1.0, scalar2=None, op0=ALU.mult), [o5])
        t_ps = s_ps[2]
        o7 = P.op("pe", lambda e: e.transpose(t_ps[0:NBB, 0:128], c_sb[:, :], idf[:, :]), [o5, L["idf"]])
        o8 = P.op("dve", lambda e: e.tensor_copy(out=ct[:, :], in_=t_ps[0:NBB, 0:128]), [o7])
        o9 = P.op("dve", lambda e: e.tensor_copy(out=chi[:, :], in_=ct[:, :]), [o8])
        o10 = P.op("dve", lambda e: e.tensor_tensor(out=r1[:, :], in0=ct[:, :], in1=chi[:, :], op=ALU.subtract), [o9])
        o11 = P.op("dve", lambda e: e.tensor_copy(out=cmi[:, :], in_=r1[:, :]), [o10])
        o12 = P.op("dve", lambda e: e.tensor_tensor(out=r2[:, :], in0=r1[:, :], in1=cmi[:, :], op=ALU.subtract), [o11])
        o13 = P.op("dve", lambda e: e.tensor_copy(out=clo[:, :], in_=r2[:, :]), [o12])
        cscr = P.dram("cscr", [3, NBB * 128], BF16, "Internal")
        cw = []
        for j, (src, dep) in enumerate(((chi, o9), (cmi, o11), (clo, o13))):
            cw.append(P.dma("sp", cscr[j].rearrange("(b t) -> b t", t=128), src[:, :], deps=[dep], slot="cscr"))
        crow_ready = P.dma("sp", crow[0:3, :], cscr[:, :], deps=[cw[-1]], slot="crow")
        pre_done = [o8]
        prolog_free = {0: o2, 1: o5, 2: o8}


    pairs = []
    for typ in 'mf':
        pairs.append(dict(typ=typ, qb=-1, q0=FR, n=16, kb=-1, first=True, last=True, diag=None))
        for j in range(NQB):
            kbs = [-1] + list(range(4 * j + 4))
            for ii, kb in enumerate(kbs):
                pairs.append(dict(typ=typ, qb=j, q0=512 * j, n=512, kb=kb, first=(ii == 0), last=(ii == len(kbs) - 1),
                                  diag=(kb - 4 * j) if kb >= 4 * j else None))
    npairs = len(pairs)

    s_done = [None] * npairs
    e_done = [None] * npairs
    pv_done = [None] * npairs
    norm_done = {}
    out_dma = {}
    qseq = -1
    qseq_of = [0] * npairs
    for i, pr in enumerate(pairs):
        if pr["first"]:
            qseq += 1
        qseq_of[i] = qseq

    def emit_S(i):
        pr = pairs[i]
        typ, q0, n, kb = pr["typ"], pr["q0"], pr["n"], pr["kb"]
        sl = i % NS
        rows = 16 if kb < 0 else 128
        k0 = FR if kb < 0 else 128 * kb
        sp_ = s_ps[sl]
        deps = []
        if i - NS >= 0:
            deps.append(e_done[i - NS])
        else:
            deps.append(prolog_free[sl])
        out = sp_[0:rows, 0:n]
        steps = []
        if typ == "m":
            steps.append((kn[:, k0:k0 + rows], qn[:, q0:q0 + n], [L["kn"], L["qn"]]))
            steps.append((kr[:, k0:k0 + rows], qr[:, q0:q0 + n], [L["kr"], L["qr"]]))
        else:
            steps.append((fk[:, k0:k0 + rows], fq[:, q0:q0 + n], [L["fk"], L["fq"]]))
            cq0 = 0 if pr["qb"] < 0 else 128 + 512 * pr["qb"]
            steps.append((onb[0:3, 0:rows], crow[0:3, cq0:cq0 + n], [L["onb"], crow_ready]))
        if pr["qb"] < 0:
            if typ == "f":
                steps.append((idb[0:16, 0:16], cmeta[0:16, 0:16], [L["idb"], L["cmeta"]]))
        elif pr["diag"] is not None:
            mk = cmf if typ == "f" else cmm
            dg = pr["diag"]
            steps.append((idb[:, :], mk[:, 512 * dg:512 * dg + 512], [L["idb"], L["cmf"], L["cmm"]]))
        last = None
        for si, (lt, rh, dd) in enumerate(steps):
            last = P.op("pe", (lambda e, out=out, lt=lt, rh=rh, a=(si == 0), b=(si == len(steps) - 1):
                               e.matmul(out, lt, rh, start=a, stop=b)), deps + dd if si == 0 else dd)
        s_done[i] = last

    def emit_E(i):
        pr = pairs[i]
        typ, n, kb = pr["typ"], pr["n"], pr["kb"]
        sl = i % NS
        rows = 16 if kb < 0 else 128
        deps = [s_done[i]]
        if i - NS >= 0:
            deps.append(pv_done[i - NS])
        if typ == "f":
            cb = 0 if kb < 0 else kb + 1
            bias = negc[0:rows, cb:cb + 1]
            deps.append(o6)
            fn = lambda e, sl=sl, rows=rows, n=n, bias=bias: e.activation(
                out=p_sb[sl][0:rows, 0:n], in_=s_ps[sl][0:rows, 0:n], func=AF.Exp, bias=bias, scale=1.0)
        else:
            fn = lambda e, sl=sl, rows=rows, n=n: e.activation(
                out=p_sb[sl][0:rows, 0:n], in_=s_ps[sl][0:rows, 0:n], func=AF.Exp)
        e_done[i] = P.op("act", fn, deps)

    def emit_PV(i):
        pr = pairs[i]
        typ, q0, n, kb = pr["typ"], pr["q0"], pr["n"], pr["kb"]
        sl = i % NS
        rows = 16 if kb < 0 else 128
        vb = NB if kb < 0 else kb
        v = vm if typ == "m" else vf
        qs = qseq_of[i]
        ob = qs % 2
        deps = [e_done[i], L["vm"], L["vf"], L["onb"]]
        if pr["first"] and qs - 2 in norm_done:
            deps.append(norm_done[qs - 2])
        a, b = pr["first"], pr["last"]
        P.op("pe", lambda e: e.matmul(o_ps[ob][:, 0:n], v[0:rows, vb * 128:vb * 128 + 128], p_sb[sl][0:rows, 0:n],
                                      start=a, stop=b), deps)
        pv_done[i] = P.op("pe", lambda e: e.matmul(d_ps[ob][:, 0:n], onb[0:rows, 0:128], p_sb[sl][0:rows, 0:n],
                                                   start=a, stop=b), [])
        if pr["last"]:
            deps2 = [pv_done[i]]
            if qs - 2 in out_dma:
                deps2.append(out_dma[qs - 2])
            r = P.op("dve", lambda e: e.reciprocal(out=rden[ob][:, 0:n], in_=d_ps[ob][:, 0:n]), deps2)
            nd = P.op("dve", lambda e: e.tensor_tensor(out=o_sb[ob][:, 0:n], in0=o_ps[ob][:, 0:n], in1=rden[ob][:, 0:n],
                                                       op=ALU.mult), [r])
            norm_done[qs] = nd
            dst = om_d if typ == "m" else of_d
            out_dma[qs] = P.dma("sp", dst[:, q0:q0 + n], o_sb[ob][:, 0:n], deps=[nd], slot="out%d" % ob)

    for i in range(npairs + 1):
        if i < npairs:
            emit_S(i)
            emit_E(i)
        if i >= 1:
            emit_PV(i - 1)
    fin = [out_dma[k] for k in sorted(out_dma)[-2:]]
    P.wait("sp", fin)
    if own:
        return P.build()
    return fin


def build_k3a(P=None, io=None):
    own = P is None
    if own:
        P = Prog()
    io = io or {}
    din = lambda name, shape, dt: io[name] if name in io else P.dram(name, shape, dt, "ExternalInput")
    xT_d = din("xT", [2048, TP], F32)
    OM_d = din("OM", [8, 128, TP], F32)
    OF_d = din("OF", [8, 128, TP], F32)
    GATE_d = din("GATE", [8, 128, TP], F32)
    wo_d = din("wo", [16, 128, 16 * 128], F32)
    g_d = din("g_post", [128, 16], F32)
    on_d = din("c_on2048", [128, 128], BF16)
    h1_d = io["h1T"] if "h1T" in io else P.dram("h1T", [2048, TP], F32, "ExternalOutput")

    mixin = P.sb("mixin", [128, 16, TP], BF16)
    mixT = P.sb("mixT", [128, 16, TP], F32)
    stf = [P.sb("stf%d" % i, [128, TP], F32) for i in range(4)]
    st_free = [[] for _ in range(4)]
    rstd = P.sb("rstd", [128, TP], F32)
    sqb = [P.sb("sqb%d" % i, [128, TP], BF16) for i in range(2)]
    sq_free = [[], []]
    gp = P.sb("gp", [128, 16], F32)
    on2048 = P.sb("on2048", [128, 128], BF16)
    epsb = P.sb("epsb", [128, 1], F32)
    pools = Pools(P)
    L = {}
    L["g"] = P.dma("sp", gp[:], g_d, slot="c_g")
    L["on"] = P.dma("sp", on2048[:], on_d, slot="c_on")
    m1 = P.op("dve", lambda e: e.memset(epsb[:], EPS), [])
    mi_ops = []
    for h in range(8):
        mi_ops.append(P.dma("pool", mixin[:, h, :], OM_d[h], slot="om"))
    si = 0
    for h in range(8):
        a = si % 4; b = (si + 1) % 4; si += 2
        la = P.dma("sp", stf[a][:, :], OF_d[h], deps=st_free[a], slot="st%d" % a)
        lb = P.dma("sp", stf[b][:, :], GATE_d[h], deps=st_free[b], slot="st%d" % b)
        o = P.op("dve", (lambda e, h=h, a=a, b=b: e.tensor_tensor(out=mixin[:, 8 + h, :], in0=stf[a][:, :], in1=stf[b][:, :],
                                                                  op=ALU.mult)), [la, lb])
        st_free[a] = [o]; st_free[b] = [o]
        mi_ops.append(o)
    mix_ready = [mi_ops[7], mi_ops[-1]]
    cp_ops = []
    for oc in range(16):
        s, pt, mm, _, _ = fm_chunk(P, pools, wo_d[oc], 16, 128, mixin, mix_ready)
        evs = []
        for ti, (c0, n) in enumerate(CT):
            evs.append(P.op("act", (lambda e, o=mixT[:, oc, c0:c0 + n], a=pt[:, ti * 512:ti * 512 + n]:
                                    e.activation(out=o, in_=a, func=AF.Copy)), [mm]))
        pools.ps_free[s] = evs
        cp_ops.append(evs[-1])
    rr = rms_stats(P, pools, [mixT[:, oc, :] for oc in range(16)], cp_ops[-1:], on2048[:, :], L["on"], sqb, sq_free, rstd,
                   epsb[:, 0:1], wdeps=[m1])
    xv = xT_d.rearrange("(kc p) t -> p kc t", p=128)
    hv = h1_d.rearrange("(kc p) t -> p kc t", p=128)
    fin = []
    for oc in range(16):
        a = si % 4; si += 1
        ld = P.dma("sp", stf[a][:, :], xv[:, oc, :], deps=st_free[a], slot="st%d" % a)
        o1 = P.op("dve", (lambda e, oc=oc: e.scalar_tensor_tensor(out=mixT[:, oc, :], in0=mixT[:, oc, :], scalar=gp[:, oc:oc + 1],
                                                                   in1=rstd[:, :], op0=ALU.mult, op1=ALU.mult)), [rr, L["g"]])
        o2 = P.op("dve", (lambda e, oc=oc, a=a: e.tensor_tensor(out=stf[a][:, :], in0=stf[a][:, :], in1=mixT[:, oc, :], op=ALU.add)),
                  [o1, ld])
        od = P.dma("sp", hv[:, oc, :], stf[a][:, :], deps=[o2], slot="ho%d" % a)
        st_free[a] = [od]
        fin.append(od)
    P.wait("sp", fin)
    if own:
        return P.build()
    return fin


def build_k3b(P=None, io=None):
    own = P is None
    if own:
        P = Prog()
    io = io or {}
    din = lambda name, shape, dt: io[name] if name in io else P.dram(name, shape, dt, "ExternalInput")
    h1_d = din("h1T", [2048, TP], F32)
    wu_d = din("wu", [44, 128, 16 * 256], F32)
    wd_d = din("wd", [16, 128, 44 * 128], F32)
    cw_d = din("cw", [128, 44 * 2 * 3], F32)
    cb_d = din("cb", [128, 44 * 2], F32)
    gpre_d = din("g_fpre", [128, 16], F32)
    gpost_d = din("g_fpost", [128, 16], F32)
    on_d = din("c_on2048", [128, 128], BF16)
    h2_d = io["h2T"] if "h2T" in io else P.dram("h2T", [2048, TP], F32, "ExternalOutput")
    fscr = P.dram("fscr", [16, 128, TP], F32, "Internal")

    hn2 = P.sb("hn2", [128, 16, TP], BF16)
    act = P.sb("actT", [128, 44, TP], BF16)
    stf = [P.sb("stf%d" % i, [128, TP], F32) for i in range(2)]
    st_free = [[] for _ in range(2)]
    rstd = P.sb("rstd", [128, TP], F32)
    ssq = P.sb("ssq", [128, TP], F32)
    sq32 = P.sb("sq32", [128, TP], F32)
    sqb = [P.sb("sqb%d" % i, [128, TP], BF16) for i in range(2)]
    sq_free = [[], []]
    gpre = P.sb("gpre", [128, 16], F32); gpost = P.sb("gpost", [128, 16], F32)
    cw = P.sb("cw_s", [128, 44 * 6], F32); cb = P.sb("cb_s", [128, 44 * 2], F32)
    on2048 = P.sb("on2048", [128, 128], BF16)
    epsb = P.sb("epsb", [128, 1], F32)
    raw = [P.sb("raw%d" % i, [128, TP + 2], F32) for i in range(2)]
    tmp = [P.sb("tmp%d" % i, [128, TP], F32) for i in range(2)]
    wds = [P.sb("wds%d" % i, [128, 22 * 128], BF16) for i in range(3)]
    wd_free = [[], [], []]
    wdi = 0
    pools = Pools(P, nps=2, nw=2, wsize=16 * 256, banks=3)
    L = {}
    for nm, s, d in (("gpre", gpre, gpre_d), ("gpost", gpost, gpost_d), ("cw", cw, cw_d), ("cb", cb, cb_d), ("on", on2048, on_d)):
        L[nm] = P.dma("sp", s[:], d, slot="c_" + nm)
    m1 = P.op("dve", lambda e: e.memset(epsb[:], EPS), [])
    hv = h1_d.rearrange("(kc p) t -> p kc t", p=128)
    ov = h2_d.rearrange("(kc p) t -> p kc t", p=128)
    s0, pt0, pd0 = pools.get_ps()
    si = 0
    last = None
    for kc in range(16):
        b = si % 2; si += 1
        ld = P.dma("sp", stf[b][:, :], hv[:, kc, :], deps=st_free[b], slot="st%d" % b)
        q = kc % 2
        sq = P.op("act", (lambda e, o=sqb[q][:, :], a=stf[b][:, :]: e.activation(out=o, in_=a, func=AF.Square)), [ld] + sq_free[q])
        st_free[b] = [sq]
        for ti, (c0, n) in enumerate(CT):
            last = P.op("pe", (lambda e, o=pt0[:, ti * 512:ti * 512 + n], r=sqb[q][:, c0:c0 + n], a=(kc == 0), bb=(kc == 15):
                               e.matmul(o, on2048[:, :], r, start=a, stop=bb)), [sq, L["on"]] + (pd0 if kc == 0 else []))
        sq_free[q] = [last]
    sops = []
    for ti, (c0, n) in enumerate(CT):
        sops.append(P.op("act", (lambda e, o=rstd[:, c0:c0 + n], a=pt0[:, ti * 512:ti * 512 + n]:
                                 e.activation(out=o, in_=a, func=AF.Sqrt, bias=epsb[:, 0:1], scale=1.0)), [last, m1]))
    pools.ps_free[s0] = list(sops)
    r0 = P.op("dve", lambda e: e.reciprocal(out=rstd[:, :], in_=rstd[:, :]), sops)
    hn_ops = []
    for kc in range(16):
        b = si % 2; si += 1
        ld = P.dma("sp", stf[b][:, :], hv[:, kc, :], deps=st_free[b], slot="st%d" % b)
        o_ = P.op("dve", (lambda e, kc=kc, b=b: e.scalar_tensor_tensor(out=hn2[:, kc, :], in0=stf[b][:, :], scalar=gpre[:, kc:kc + 1],
                                                                        in1=rstd[:, :], op0=ALU.mult, op1=ALU.mult)), [r0, L["gpre"], ld])
        st_free[b] = [o_]
        hn_ops.append(o_)
    hn_ready = [hn_ops[-1]]

    raw_free = [[], []]
    tmp_free = [[], []]
    z0 = P.op("dve", lambda e: e.memset(raw[0][:, 0:2], 0.0), [])
    z1 = P.op("dve", lambda e: e.memset(raw[1][:, 0:2], 0.0), [])
    zdep = [[z0], [z1]]
    act_w = []
    for i in range(44):
        res = []
        wl = ws = None
        for gu in range(2):
            s_, pt, mm, wl, ws = fm_chunk(P, pools, wu_d[i] if gu == 0 else None, 16, 128, hn2, hn_ready, m0=2048 * gu,
                                          wl=wl, ws=ws, cts=CT, wcols=4096)
            r = raw[gu]
            evs = []
            for ti, (c0, n) in enumerate(CT):
                evs.append(P.op("act", (lambda e, o=r[:, 2 + c0:2 + c0 + n], a=pt[:, ti * 512:ti * 512 + n]:
                                        e.activation(out=o, in_=a, func=AF.Copy)),
                                [mm] + ((raw_free[gu] + zdep[gu]) if ti == 0 else [])))
            pools.ps_free[s_] = evs
            res.append(evs)
        for gu in range(2):
            evs = res[gu]
            t = tmp[gu]
            r = raw[gu]
            ci = i * 2 + gu
            w0 = cw[:, ci * 3 + 0:ci * 3 + 1]
            w1 = cw[:, ci * 3 + 1:ci * 3 + 2]
            w2 = cw[:, ci * 3 + 2:ci * 3 + 3]
            bb = cb[:, ci:ci + 1]
            c1 = P.op("dve", (lambda e, t=t, r=r, w2=w2, bb=bb: e.tensor_scalar(out=t[:, 0:TP], in0=r[:, 2:2 + TP], scalar1=w2, scalar2=bb,
                                                                              op0=ALU.mult, op1=ALU.add)),
                      evs + tmp_free[gu] + [L["cw"], L["cb"]])
            NF = 1032
            c2 = P.op("dve", (lambda e, t=t, r=r, w1=w1: e.scalar_tensor_tensor(out=t[:, 0:NF], in0=r[:, 1:1 + NF], scalar=w1, in1=t[:, 0:NF],
                                                                             op0=ALU.mult, op1=ALU.add)), [c1])
            c3 = P.op("dve", (lambda e, t=t, r=r, w0=w0: e.scalar_tensor_tensor(out=t[:, 0:NF], in0=r[:, 0:NF], scalar=w0, in1=t[:, 0:NF],
                                                                             op0=ALU.mult, op1=ALU.add)), [c2])
            c4 = P.op("dve", (lambda e, t=t, r=r, w1=w1: e.scalar_tensor_tensor(out=t[:, 1033:1048], in0=r[:, 1034:1049], scalar=w1, in1=t[:, 1033:1048],
                                                                             op0=ALU.mult, op1=ALU.add)), [c3])
            c5 = P.op("dve", (lambda e, t=t, r=r, w0=w0: e.scalar_tensor_tensor(out=t[:, 1034:1048], in0=r[:, 1034:1048], scalar=w0, in1=t[:, 1034:1048],
                                                                             op0=ALU.mult, op1=ALU.add)), [c4])
            raw_free[gu] = [c5]
            res[gu] = c5
        g_ = P.op("act", lambda e: e.activation(out=tmp[0][:, :], in_=tmp[0][:, :], func=AF.Gelu_apprx_tanh), [res[0]])
        a_ = P.op("dve", (lambda e, i=i: e.tensor_tensor(out=act[:, i, :], in0=tmp[0][:, :], in1=tmp[1][:, :], op=ALU.mult)),
                  [g_, res[1]])
        tmp_free[0] = [a_]; tmp_free[1] = [a_]
        act_w.append(a_)
    fw_ops = []
    ssq_prev = []
    sq32_free = []
    for oc in range(16):
        wsl = []
        wls = []
        for hf in range(2):
            w = wdi % 3; wdi += 1
            wls.append(P.dma("pool", wds[w][:, :], wd_d[oc][:, hf * 22 * 128:(hf + 1) * 22 * 128], deps=wd_free[w], slot="wd%d" % w))
            wsl.append(w)
        s_, pt, pd = pools.get_ps()
        last = None
        for ti, (c0, n) in enumerate(CT):
            for ch in range(44):
                deps = (wls + act_w[-1:] + pd) if (ti == 0 and ch == 0) else []
                wt_ = wds[wsl[ch // 22]]
                cc = (ch % 22) * 128
                last = P.op("pe", (lambda e, o=pt[:, ti * 512:ti * 512 + n], l=wt_[:, cc:cc + 128],
                                   r=act[:, ch, c0:c0 + n], a=(ch == 0), b=(ch == 43): e.matmul(o, l, r, start=a, stop=b)), deps)
        for w in wsl:
            wd_free[w] = [last]
        b = si % 2; si += 1
        evs = []
        for ti, (c0, n) in enumerate(CT):
            evs.append(P.op("act", (lambda e, o=stf[b][:, c0:c0 + n], a=pt[:, ti * 512:ti * 512 + n]:
                                    e.activation(out=o, in_=a, func=AF.Copy)), [last] + (st_free[b] if ti == 0 else [])))
        pools.ps_free[s_] = evs
        od = P.dma("sp", fscr[oc], stf[b][:, :], deps=evs, slot="fs%d" % b)
        if oc == 0:
            sqo = P.op("act", (lambda e, b=b: e.activation(out=ssq[:, :], in_=stf[b][:, :], func=AF.Square)), evs)
            ssq_prev = [sqo]
            st_free[b] = [od, sqo]
        else:
            sqo = P.op("act", (lambda e, b=b: e.activation(out=sq32[:, :], in_=stf[b][:, :], func=AF.Square)), evs + sq32_free)
            ad = P.op("dve", lambda e: e.tensor_tensor(out=ssq[:, :], in0=ssq[:, :], in1=sq32[:, :], op=ALU.add), [sqo] + ssq_prev)
            ssq_prev = [ad]
            sq32_free = [ad]
            st_free[b] = [od, sqo]
        fw_ops.append(od)
    cvt = P.op("dve", lambda e: e.tensor_copy(out=sqb[0][:, :], in_=ssq[:, :]), ssq_prev + sq_free[0])
    s0, pt0, pd0 = pools.get_ps()
    last = None
    for ti, (c0, n) in enumerate(CT):
        last = P.op("pe", (lambda e, o=pt0[:, ti * 512:ti * 512 + n], r=sqb[0][:, c0:c0 + n]:
                           e.matmul(o, on2048[:, :], r, start=True, stop=True)), [cvt] + pd0)
    sops = []
    for ti, (c0, n) in enumerate(CT):
        sops.append(P.op("act", (lambda e, o=rstd[:, c0:c0 + n], a=pt0[:, ti * 512:ti * 512 + n]:
                                 e.activation(out=o, in_=a, func=AF.Sqrt, bias=epsb[:, 0:1], scale=1.0)), [last] + hn_ops[-1:]))
    r1 = P.op("dve", lambda e: e.reciprocal(out=rstd[:, :], in_=rstd[:, :]), sops)
    stg = [stf[0], stf[1], raw[0], raw[1]]
    stg_free = [list(st_free[0]), list(st_free[1]), list(raw_free[0]), list(raw_free[1])]
    fin = []
    for oc in range(16):
        a = (2 * oc) % 4; b = (2 * oc + 1) % 4
        lf_ = P.dma("sp", stg[a][:, 0:TP], fscr[oc], deps=stg_free[a] + [fw_ops[oc]], slot="rf%d" % a)
        lh_ = P.dma("sp", stg[b][:, 0:TP], hv[:, oc, :], deps=stg_free[b], slot="rf%d" % b)
        o1 = P.op("dve", (lambda e, oc=oc, a=a: e.scalar_tensor_tensor(out=stg[a][:, 0:TP], in0=stg[a][:, 0:TP], scalar=gpost[:, oc:oc + 1],
                                                                        in1=rstd[:, :], op0=ALU.mult, op1=ALU.mult)), [r1, L["gpost"], lf_])
        o2 = P.op("dve", (lambda e, a=a, b=b: e.tensor_tensor(out=stg[b][:, 0:TP], in0=stg[b][:, 0:TP], in1=stg[a][:, 0:TP], op=ALU.add)), [o1, lh_])
        od = P.dma("sp", ov[:, oc, :], stg[b][:, 0:TP], deps=[o2], slot="ho%d" % b)
        stg_free[a] = [o2]
        stg_free[b] = [od]
        fin.append(od)
    P.wait("sp", fin)
    if own:
        return P.build()
    return fin


import ml_dtypes
bf = ml_dtypes.bfloat16
TP = 1048
def fmN(w, nk):
    n = w.shape[1]
    if n < 128: w = np.pad(w, ((0, 0), (0, 128 - n)))
    return np.ascontiguousarray(w.reshape(nk, 128, 128).transpose(1, 0, 2).reshape(128, nk * 128))
def tmN(w, nk):
    return np.ascontiguousarray(w.reshape(nk, 128, 512).transpose(1, 0, 2).reshape(128, nk * 512))
def gcol(g):
    return np.ascontiguousarray(g.reshape(-1, 128).T.astype(np.float32))
def prep_k1_weights(inp, l):
    w_in = np.asarray(inp["w_in"][l])
    ch = []
    for j in range(4): ch.append(fmN(w_in[:, 128 * j:128 * j + 128], 16))
    for j in range(4): ch.append(fmN(w_in[:, 512 + 128 * j:512 + 128 * j + 128], 16))
    kr = 1024 + np.arange(64); krs = np.concatenate([kr[32:], kr[:32]])
    ch.append(fmN(w_in[:, np.concatenate([kr, krs])], 16))
    for base in (1088, 2112, 4160):
        for h in range(8): ch.append(fmN(w_in[:, base + 128 * h:base + 128 * h + 128], 16))
    ch.append(fmN(w_in[:, 5184:5192], 16))
    d = {"w1": np.stack(ch)}
    d["w1v"] = np.stack([tmN(w_in[:, 3136 + 512 * g:3136 + 512 * g + 512], 16) for g in range(2)])
    wq_ = np.asarray(inp["w_q_up"][l]); wq = []
    for h in range(8):
        wq.append(fmN(wq_[:, 192 * h:192 * h + 128], 4))
        r = 192 * h + 128 + np.arange(64); rs = np.concatenate([r[32:], r[:32]])
        wq.append(fmN(wq_[:, np.concatenate([r, rs])], 4))
    d["wq"] = np.stack(wq)
    wkv = np.asarray(inp["w_kv_up"][l])
    d["wk"] = np.stack([fmN(wkv[:, 256 * h:256 * h + 128], 4) for h in range(8)])
    wv = []
    for g in range(2):
        cols = np.concatenate([256 * h + 128 + np.arange(128) for h in range(4 * g, 4 * g + 4)])
        wv.append(tmN(wkv[:, cols], 4))
    d["wv"] = np.stack(wv)
    d["g_pre"] = gcol(np.asarray(inp["ln_mix_pre"][l])); d["g_ql"] = gcol(np.asarray(inp["g_q_latent"][l]))
    d["g_kvl"] = gcol(np.asarray(inp["g_kv_latent"][l]))
    d["g_fq"] = gcol(np.asarray(inp["g_fox_q"][l])); d["g_fk"] = gcol(np.asarray(inp["g_fox_k"][l]))
    d["b_f"] = np.asarray(inp["b_forget"][l]).reshape(8, 1).astype(np.float32)
    return d
def consts_k1():
    return {"c_on2048": np.full((128, 128), 1 / 2048, np.float32).astype(bf),
            "c_on512": np.full((128, 128), 1 / 512, np.float32).astype(bf),
            "c_on128": np.full((128, 128), 1 / 128, np.float32).astype(bf)}
def core_pos(c):
    halo = np.arange(8, 16) if c == 0 else 16 + np.arange(1024 * c - 8, 1024 * c)
    return np.concatenate([halo, 16 + 1024 * c + np.arange(1024), np.arange(16)])
def rope_tables(c):
    pos = core_pos(c).astype(np.float32)
    inv = (np.float32(10000.0) ** (-np.arange(32, dtype=np.float32) / np.float32(32))).astype(np.float32)
    ang = (pos[:, None] * inv[None, :]).astype(np.float32)
    cs, sn = np.cos(ang).T.astype(np.float32), np.sin(ang).T.astype(np.float32)
    return {"cos2": np.ascontiguousarray(np.concatenate([cs, cs])), "sin2": np.ascontiguousarray(np.concatenate([-sn, sn]))}
def core_x(inp, c):
    x = np.asarray(inp["x"])[0]; m = np.asarray(inp["meta_tokens"])
    halo = m[8:16] if c == 0 else x[1024 * c - 8:1024 * c]
    return np.ascontiguousarray(np.concatenate([halo, x[1024 * c:1024 * c + 1024], m]).T.astype(np.float32))
def prep_k3a_weights(inp, l):
    wo = np.asarray(inp["w_out"][l])
    return {"wo": np.stack([fmN(wo[:, 128 * oc:128 * oc + 128], 16) for oc in range(16)]),
            "g_post": gcol(np.asarray(inp["ln_mix_post"][l]))}
def prep_k3b_weights(inp, l):
    wu = np.asarray(inp["w_ffn_up"][l]); wd = np.asarray(inp["w_ffn_down"][l])
    wc = np.asarray(inp["w_ffn_conv"][l]); bc = np.asarray(inp["b_ffn_conv"][l])
    d = {}
    d["wu"] = np.stack([np.concatenate([fmN(wu[:, 128 * i:128 * i + 128], 16), fmN(wu[:, 5632 + 128 * i:5632 + 128 * i + 128], 16)], axis=1)
                        for i in range(44)])
    d["wd"] = np.stack([np.ascontiguousarray(wd[:, 128 * oc:128 * oc + 128].reshape(44, 128, 128).transpose(1, 0, 2).reshape(128, 44 * 128))
                        for oc in range(16)])
    cw = wc.reshape(3, 2, 44, 128).transpose(3, 2, 1, 0)
    d["cw"] = np.ascontiguousarray(cw.reshape(128, 44 * 6)).astype(np.float32)
    d["cb"] = np.ascontiguousarray(bc.reshape(2, 44, 128).transpose(2, 1, 0).reshape(128, 88)).astype(np.float32)
    d["g_fpre"] = gcol(np.asarray(inp["ln_ffn_pre"][l])); d["g_fpost"] = gcol(np.asarray(inp["ln_ffn_post"][l]))
    return d


def _run(nc, in_maps):
    return run_bass_kernel_spmd(nc, in_maps, core_ids=list(range(8))).results


def _gather_fm(R1, key, hh, rows=slice(None)):
    parts = [np.asarray(R1[c][key][hh])[rows, 8:1032] for c in range(8)] + [np.asarray(R1[0][key][hh])[rows, 1032:1048]]
    return np.ascontiguousarray(np.concatenate(parts, axis=1))


def _vlay(v):
    out = np.zeros((128, 65, 128), v.dtype)
    out[:, :64, :] = v[:8192].reshape(64, 128, 128).transpose(1, 0, 2)
    out[:16, 64, :] = v[8192:]
    return np.ascontiguousarray(out.reshape(128, 65 * 128))


def _k2_inputs(R1, hh, c2):
    d = dict(c2)
    d["qn"] = _gather_fm(R1, "QM", hh, slice(0, 128)); d["qr"] = _gather_fm(R1, "QM", hh, slice(128, 192))
    d["kn"] = _gather_fm(R1, "KN", hh)
    d["kr"] = np.ascontiguousarray(np.concatenate([np.asarray(R1[c]["KR"])[:, 8:1032] for c in range(8)]
                                                  + [np.asarray(R1[0]["KR"])[:, 1032:1048]], axis=1))
    d["fq"] = _gather_fm(R1, "FQ", hh); d["fk"] = _gather_fm(R1, "FK", hh)
    for key, nm in (("VM", "vm"), ("VF", "vf")):
        v = np.concatenate([np.asarray(R1[c][key])[8:1032, 128 * hh:128 * hh + 128] for c in range(8)]
                           + [np.asarray(R1[0][key])[1032:1048, 128 * hh:128 * hh + 128]], axis=0)
        d[nm] = _vlay(v)
    lfv = np.concatenate([np.asarray(R1[c]["LOGF"])[hh, 8:1032] for c in range(8)]).astype(np.float32)
    lfm = np.asarray(R1[0]["LOGF"])[hh, 1032:1048].astype(np.float32)
    lf = np.zeros((128, 65), np.float32)
    lf[:16, 0] = lfm
    lf[:, 1:] = lfv.reshape(64, 128).T
    d["lf"] = lf
    return d


def _tok_slice(a, c):
    halo = a[..., 8192 + 8:8192 + 16] if c == 0 else a[..., 1024 * c - 8:1024 * c]
    return np.concatenate([halo, a[..., 1024 * c:1024 * c + 1024], a[..., 8192:8208]], axis=-1)


def kernel(**inp):
    inp = {k: np.asarray(v) for k, v in inp.items()}
    nc1 = build_k1(); nc2 = build_k2(); nc3a = build_k3a(); nc3b = build_k3b()
    c1 = consts_k1(); c2 = k2_consts()
    con = {"c_on2048": c1["c_on2048"]}
    ropes = [rope_tables(c) for c in range(8)]
    hT = [core_x(inp, c) for c in range(8)]
    for l in range(4):
        w1 = prep_k1_weights(inp, l)
        R1 = _run(nc1, [dict(w1, **c1, **ropes[c], xT=hT[c]) for c in range(8)])
        del w1
        R2 = _run(nc2, [_k2_inputs(R1, hh, c2) for hh in range(8)])
        om = np.stack([np.asarray(R2[hh]["om"]) for hh in range(8)])
        of = np.stack([np.asarray(R2[hh]["of"]) for hh in range(8)])
        w3a = prep_k3a_weights(inp, l)
        R3a = _run(nc3a, [dict(w3a, **con, xT=hT[c], OM=np.ascontiguousarray(_tok_slice(om, c)),
                               OF=np.ascontiguousarray(_tok_slice(of, c)), GATE=np.asarray(R1[c]["GATE"])) for c in range(8)])
        del w3a, R1, R2, om, of
        w3b = prep_k3b_weights(inp, l)
        R3b = _run(nc3b, [dict(w3b, **con, h1T=np.asarray(R3a[c]["h1T"])) for c in range(8)])
        del w3b
        hT = [np.asarray(R3b[c]["h2T"]) for c in range(8)]
    out = np.concatenate([hT[c][:, 8:1032].T for c in range(8)], axis=0)[None]
    return np.ascontiguousarray(out.astype(np.float32))
```
